# Optimizing a Trainium2 kernel written in Bass

```python
import math
import jax, jax.numpy as jnp
from jax import lax
import numpy as np

D_MODEL = 1024
BATCH = 8
SEQ = 4096
DEPTH = 1

CTX_LEN = 256
GRID_W = 64

RET_HEADS = 4
RET_DK = 128
RET_DV = 128
RET_QK = RET_HEADS * RET_DK
RET_V = RET_HEADS * RET_DV
DIFF_HEADS = 4
DIFF_HD = 64
DIFF_DV = 2 * DIFF_HD
DIFF_QK = DIFF_HEADS * 2 * DIFF_HD
DIFF_V = DIFF_HEADS * DIFF_DV
MIX_WIDTH = RET_V + DIFF_V
IN_COLS = 2 * RET_QK + 2 * RET_V + 2 * DIFF_QK + DIFF_V
D_FF = 2816
CONV_WIDTH = 3
CHUNK = 128
Q_BLOCK = 128
ROPE_BASE = 10000.0
EPS = 1e-6
GN_EPS = 1e-5
N_MOD = 6

kernel_name = "hybrid_retention_diffattn_dit_layer"


def rms_norm(x, g):
    xf = x.astype(jnp.float32)
    y = xf * lax.rsqrt(jnp.mean(xf * xf, axis=-1, keepdims=True) + EPS)
    return (y * g.astype(jnp.float32)).astype(x.dtype)


def grid_rope(rows, head_dim):
    n_freq = head_dim // 4
    inv = ROPE_BASE ** (-jnp.arange(n_freq, dtype=jnp.float32) / n_freq)
    pos = jnp.arange(rows * GRID_W)
    row = (pos // GRID_W).astype(jnp.float32)
    col = (pos % GRID_W).astype(jnp.float32)
    ang = jnp.concatenate([row[:, None] * inv, col[:, None] * inv], axis=-1)
    return jnp.cos(ang), jnp.sin(ang)


def apply_rope(x, cos, sin):
    half = x.shape[-1] // 2
    x1, x2 = x[..., :half], x[..., half:]
    cs = cos[None, :, None, :].astype(x.dtype)
    sn = sin[None, :, None, :].astype(x.dtype)
    return jnp.concatenate([x1 * cs - x2 * sn, x1 * sn + x2 * cs], axis=-1)


def retention_chunked(q, k, v, log_gamma, state0):
    B, H, L, dk = q.shape
    dv = v.shape[-1]
    n = L // CHUNK
    idx = jnp.arange(CHUNK, dtype=jnp.float32)
    rel = idx[:, None] - idx[None, :]
    intra = jnp.where(rel >= 0, jnp.exp(jnp.maximum(rel, 0.0) * log_gamma[:, None, None]), 0.0)
    q_dec = jnp.exp((idx + 1.0) * log_gamma[:, None])
    k_dec = jnp.exp((CHUNK - 1.0 - idx) * log_gamma[:, None])
    c_dec = jnp.exp(CHUNK * log_gamma)

    def to_chunks(t):
        return jnp.moveaxis(t.reshape(B, H, n, CHUNK, t.shape[-1]), 2, 0)

    def step(S, inp):
        qc, kc, vc = inp
        scores = jnp.einsum('bhid,bhjd->bhij', qc, kc) * intra
        o = (jnp.einsum('bhij,bhje->bhie', scores, vc)
             + jnp.einsum('bhid,bhde->bhie', qc * q_dec[..., None], S))
        S = S * c_dec[:, None, None] + jnp.einsum('bhjd,bhje->bhde', kc * k_dec[..., None], vc)
        return S, o

    S, o = lax.scan(step, state0, (to_chunks(q), to_chunks(k), to_chunks(v)))
    o = jnp.moveaxis(o, 0, 2).reshape(B, H, L, dv)
    return o, S


def head_group_norm(o, g):
    B, H, L, dv = o.shape
    mu = jnp.mean(o, axis=-1, keepdims=True)
    var = jnp.mean(jnp.square(o - mu), axis=-1, keepdims=True)
    on = (o - mu) * lax.rsqrt(var + GN_EPS)
    return on.transpose(0, 2, 1, 3).reshape(B, L, H * dv) * g.astype(jnp.float32)


def diff_attention(q, k, v, lam):
    B, H, _, Lq, d = q.shape
    dv = v.shape[-1]
    scale = d ** -0.5

    def attend(qb):
        s = jnp.einsum('bhmqd,bhmkd->bhmqk', qb, k).astype(jnp.float32) * scale
        p = jax.nn.softmax(s, axis=-1)
        a = p[:, :, 0] - lam * p[:, :, 1]
        return jnp.einsum('bhqk,bhke->bhqe', a.astype(v.dtype), v)

    if Lq <= Q_BLOCK:
        return attend(q)
    nb = Lq // Q_BLOCK
    qb = jnp.moveaxis(q.reshape(B, H, 2, nb, Q_BLOCK, d), 3, 0)
    o = lax.map(attend, qb)
    return jnp.moveaxis(o, 0, 2).reshape(B, H, Lq, dv)


def diff_subln(o, g, lambda_init):
    B, H, L, dv = o.shape
    of = o.astype(jnp.float32)
    n = of * lax.rsqrt(jnp.mean(of * of, axis=-1, keepdims=True) + EPS)
    n = n.transpose(0, 2, 1, 3).reshape(B, L, H * dv) * g.astype(jnp.float32) * (1.0 - lambda_init)
    return n.astype(o.dtype)


def split_heads(p, rope_r, rope_d):
    B, L, _ = p.shape
    cuts = np.cumsum([RET_QK, RET_QK, RET_V, RET_V, DIFF_QK, DIFF_QK]).tolist()
    rq, rk, rv, rg, dq, dk, dvv = jnp.split(p, cuts, axis=-1)
    rq = rq.reshape(B, L, RET_HEADS, RET_DK)
    rk = rk.reshape(B, L, RET_HEADS, RET_DK)
    dq = dq.reshape(B, L, 2 * DIFF_HEADS, DIFF_HD)
    dk = dk.reshape(B, L, 2 * DIFF_HEADS, DIFF_HD)
    if rope_r is not None:
        rq = apply_rope(rq, *rope_r)
        rk = apply_rope(rk, *rope_r)
        dq = apply_rope(dq, *rope_d)
        dk = apply_rope(dk, *rope_d)
    rq = rq.transpose(0, 2, 1, 3)
    rk = rk.transpose(0, 2, 1, 3) * (RET_DK ** -0.5)
    rv = rv.reshape(B, L, RET_HEADS, RET_DV).transpose(0, 2, 1, 3)
    dq = dq.reshape(B, L, DIFF_HEADS, 2, DIFF_HD).transpose(0, 2, 3, 1, 4)
    dk = dk.reshape(B, L, DIFF_HEADS, 2, DIFF_HD).transpose(0, 2, 3, 1, 4)
    dvv = dvv.reshape(B, L, DIFF_HEADS, DIFF_DV).transpose(0, 2, 1, 3)
    return (rq, rk, rv, rg), (dq, dk, dvv)


def token_mixers(h_lat, h_ctx, w_in, decay_logit, gn_g, lam_par, subln_g, w_out,
                 lambda_init, rope_r, rope_d, need_ctx):
    dtype = h_lat.dtype
    B = h_lat.shape[0]
    (rq_l, rk_l, rv_l, rg_l), (dq_l, dk_l, dv_l) = split_heads(h_lat @ w_in, rope_r, rope_d)
    (rq_c, rk_c, rv_c, rg_c), (dq_c, dk_c, dv_c) = split_heads(h_ctx @ w_in, None, None)

    log_gamma = jax.nn.log_sigmoid(decay_logit.astype(jnp.float32))
    ret_lat = 0.0
    ret_ctx = 0.0
    for d in range(2):
        def orient(t):
            return jnp.flip(t, axis=2) if d == 1 else t
        qc, kc, vc = [orient(t.astype(jnp.float32)) for t in (rq_c, rk_c, rv_c)]
        ql, kl, vl = [orient(t.astype(jnp.float32)) for t in (rq_l, rk_l, rv_l)]
        s0 = jnp.zeros((B, RET_HEADS, RET_DK, RET_DV), jnp.float32)
        o_c, s_c = retention_chunked(qc, kc, vc, log_gamma[d], s0)
        o_l, _ = retention_chunked(ql, kl, vl, log_gamma[d], s_c)
        ret_lat = ret_lat + head_group_norm(orient(o_l), gn_g[d])
        if need_ctx:
            ret_ctx = ret_ctx + head_group_norm(orient(o_c), gn_g[d])
    y_ret_lat = jax.nn.silu(rg_l) * ret_lat.astype(dtype)

    lp = lam_par.astype(jnp.float32)
    lam = jnp.exp(jnp.sum(lp[0] * lp[1])) - jnp.exp(jnp.sum(lp[2] * lp[3])) + lambda_init
    k_all = jnp.concatenate([dk_c, dk_l], axis=3)
    v_all = jnp.concatenate([dv_c, dv_l], axis=2)
    y_diff_lat = diff_subln(diff_attention(dq_l, k_all, v_all, lam), subln_g, lambda_init)

    out_lat = jnp.concatenate([y_ret_lat, y_diff_lat], axis=-1) @ w_out
    out_ctx = None
    if need_ctx:
        y_ret_ctx = jax.nn.silu(rg_c) * ret_ctx.astype(dtype)
        y_diff_ctx = diff_subln(diff_attention(dq_c, dk_c, dv_c, lam), subln_g, lambda_init)
        out_ctx = jnp.concatenate([y_ret_ctx, y_diff_ctx], axis=-1) @ w_out
    return out_lat, out_ctx


def conv_ffn(h, w_up, conv_w, conv_b, w_down):
    u = h @ w_up
    up = jnp.pad(u, ((0, 0), (1, 1), (0, 0)))
    u = up[:, :-2] * conv_w[0] + up[:, 1:-1] * conv_w[1] + up[:, 2:] * conv_w[2] + conv_b
    a, b = jnp.split(u, 2, axis=-1)
    return (jax.nn.silu(a) * b) @ w_down


def setup_inputs(seed: int = 0) -> dict:
    key = jax.random.key(seed)
    ks = jax.random.split(key, 20)
    f32 = jnp.float32
    D = D_MODEL
    a = 5.0 + jnp.arange(RET_HEADS, dtype=f32)
    base_logit = jnp.log(2.0 ** a - 1.0)
    return {
        "x": jax.random.normal(ks[0], (BATCH, SEQ, D), f32),
        "c": jax.random.normal(ks[1], (BATCH, D), f32),
        "ctx": jax.random.normal(ks[2], (BATCH, CTX_LEN, D), f32),
        "c_ctx": jax.random.normal(ks[3], (D,), f32),
        "w_mod": jax.random.normal(ks[4], (DEPTH, D, N_MOD * D), f32) * (0.5 * D ** -0.5),
        "b_mod": jax.random.normal(ks[5], (DEPTH, N_MOD * D), f32) * 0.02,
        "norm1_g": 1.0 + 0.02 * jax.random.normal(ks[6], (DEPTH, D), f32),
        "w_in": jax.random.normal(ks[7], (DEPTH, D, IN_COLS), f32) * D ** -0.5,
        "ret_decay_logit": base_logit[None, None, :] + 0.1 * jax.random.normal(ks[8], (DEPTH, 2, RET_HEADS), f32),
        "ret_gn_g": 1.0 + 0.02 * jax.random.normal(ks[9], (DEPTH, 2, RET_V), f32),
        "diff_lambda": 0.1 * jax.random.normal(ks[10], (DEPTH, 4, DIFF_HD), f32),
        "diff_subln_g": 1.0 + 0.02 * jax.random.normal(ks[11], (DEPTH, DIFF_V), f32),
        "w_out": jax.random.normal(ks[12], (DEPTH, MIX_WIDTH, D), f32) * MIX_WIDTH ** -0.5,
        "norm2_g": 1.0 + 0.02 * jax.random.normal(ks[13], (DEPTH, D), f32),
        "w_up": jax.random.normal(ks[14], (DEPTH, D, 2 * D_FF), f32) * D ** -0.5,
        "conv_w": jax.random.normal(ks[15], (DEPTH, CONV_WIDTH, 2 * D_FF), f32) * CONV_WIDTH ** -0.5,
        "conv_b": 0.02 * jax.random.normal(ks[16], (DEPTH, 2 * D_FF), f32),
        "w_down": jax.random.normal(ks[17], (DEPTH, D_FF, D), f32) * D_FF ** -0.5,
        "final_g": 1.0 + 0.02 * jax.random.normal(ks[18], (D,), f32),
    }


def reference(x, c, ctx, c_ctx, w_mod, b_mod, norm1_g, w_in, ret_decay_logit, ret_gn_g,
              diff_lambda, diff_subln_g, w_out, norm2_g, w_up, conv_w, conv_b, w_down, final_g):
    L = x.shape[1]
    rows = L // GRID_W
    rope_r = grid_rope(rows, RET_DK)
    rope_d = grid_rope(rows, DIFF_HD)
    sc = jax.nn.silu(c)
    scc = jax.nn.silu(c_ctx)
    for l in range(DEPTH):
        last = l == DEPTH - 1
        lambda_init = 0.8 - 0.6 * math.exp(-0.3 * l)
        m_lat = sc @ w_mod[l] + b_mod[l]
        m_ctx = scc @ w_mod[l] + b_mod[l]
        sh1, sc1, g1, sh2, sc2, g2 = [t[:, None, :] for t in jnp.split(m_lat, N_MOD, axis=-1)]
        csh1, csc1, cg1, csh2, csc2, cg2 = jnp.split(m_ctx, N_MOD, axis=-1)

        h_lat = rms_norm(x, norm1_g[l]) * (1.0 + sc1) + sh1
        h_ctx = rms_norm(ctx, norm1_g[l]) * (1.0 + csc1) + csh1
        y_lat, y_ctx = token_mixers(h_lat, h_ctx, w_in[l], ret_decay_logit[l], ret_gn_g[l],
                                    diff_lambda[l], diff_subln_g[l], w_out[l], lambda_init,
                                    rope_r, rope_d, need_ctx=not last)
        x = x + g1 * y_lat
        h2 = rms_norm(x, norm2_g[l]) * (1.0 + sc2) + sh2
        x = x + g2 * conv_ffn(h2, w_up[l], conv_w[l], conv_b[l], w_down[l])
        if not last:
            ctx = ctx + cg1 * y_ctx
            h2c = rms_norm(ctx, norm2_g[l]) * (1.0 + csc2) + csh2
            ctx = ctx + cg2 * conv_ffn(h2c, w_up[l], conv_w[l], conv_b[l], w_down[l])
    return rms_norm(x, final_g)
```

```python
import numpy as np
import concourse.bass as bass
import concourse.mybir as mybir
from concourse.bass_utils import run_bass_kernel_spmd

F32 = mybir.dt.float32
BF16 = mybir.dt.bfloat16
AF = mybir.ActivationFunctionType
ALU = mybir.AluOpType
AX = mybir.AxisListType

D = 1024
L = 4096
LC = 256
NB = 8
DFF = 2816
NF = 22
EPS = 1e-6
GN_EPS = 1e-5
LAMBDA_INIT = 0.8 - 0.6 * 1.0
KT_ALL = 34
DEBUG = False
SERIAL = False
PROFILE_SCOPES = False
STOP = None


class _Stop(Exception):
    pass


class _Rec:
    def __getattr__(self, name):
        def f(*a, **kw):
            return (name, a, kw)
        return f


R = _Rec()


class Prog:
    def __init__(self, nc):
        self.nc = nc
        self.ops = []
        self.alias = {}
        self.phase = None

    def add(self, eng, fn, reads=(), writes=(), dma=None):
        if SERIAL is True or (SERIAL and (eng in SERIAL or (eng == 'pool' and ('pooldma' if dma else 'poolcmp') in SERIAL))):
            writes = tuple(writes) + ('__GLOBAL__',)
        self.ops.append(dict(eng=eng, fn=fn, reads=tuple(reads), writes=tuple(writes), dma=dma, phase=self.phase))

    def _exp(self, names):
        out = []
        for n in names:
            base = n.split(':')[0]
            if base in self.alias:
                out.extend(self.alias[base])
            else:
                out.append(n)
        return out

    def emit(self):
        nc = self.nc
        engs = {'pe': nc.tensor, 'act': nc.scalar, 'dve': nc.vector, 'pool': nc.gpsimd, 'sp': nc.sync}
        ops = self.ops
        last_w = {}
        readers = {}
        for i, o in enumerate(ops):
            deps = set()
            rds = self._exp(o['reads'])
            wrs = self._exp(o['writes'])
            for r in rds:
                if r in last_w:
                    deps.add(last_w[r])
                if r[0] == 'B':
                    for (re_, rd_), j in readers.get(r, {}).items():
                        if re_ != o['eng']:
                            deps.add(j)
            for w in wrs:
                if w in last_w:
                    deps.add(last_w[w])
                for j in readers.get(w, {}).values():
                    deps.add(j)
            deps.discard(i)
            o['deps'] = deps
            for r in rds:
                readers.setdefault(r, {})[(o['eng'], o['dma'])] = i
            for w in wrs:
                last_w[w] = i
                readers[w] = {}
        signal = [False] * len(ops)
        for i, o in enumerate(ops):
            nd = set()
            for j in o['deps']:
                p = ops[j]
                if p['dma'] is None and o['dma'] is None and p['eng'] == 'pe' and o['eng'] == 'pe':
                    continue
                nd.add(j)
                signal[j] = True
            o['deps'] = nd
        esem = {k: nc.alloc_semaphore('s_' + k) for k in engs}
        ecnt = {k: 0 for k in engs}
        dsem = {}
        dcnt = {}
        waited = {k: {} for k in engs}
        ev = [None] * len(ops)
        final = {}
        cur_scope = None
        for i, o in enumerate(ops):
            if PROFILE_SCOPES and o['phase'] != (cur_scope[0] if cur_scope else None):
                if cur_scope:
                    nc.leave_named_scope(cur_scope[0], cur_scope[1], False)
                    cur_scope = None
                if o['phase']:
                    sid, _ = nc.enter_named_scope(o['phase'], False)
                    cur_scope = (o['phase'], sid)
            E = o['eng']
            e = engs[E]
            need = {}
            for j in o['deps']:
                s, v, nm = ev[j]
                if need.get(nm, (None, 0))[1] < v:
                    need[nm] = (s, v)
            for nm, (s, v) in need.items():
                if waited[E].get(nm, 0) >= v:
                    continue
                e.wait_ge(s, v)
                waited[E][nm] = v
            name, a_, kw_ = o['fn']
            inst = getattr(e, name)(*a_, **kw_)
            if o['dma'] is not None:
                k = o['dma']
                if k not in dsem:
                    dsem[k] = nc.alloc_semaphore('d_' + k)
                    dcnt[k] = 0
                dcnt[k] += 16
                inst.then_inc(dsem[k], 16)
                ev[i] = (dsem[k], dcnt[k], 'd_' + k)
                final['d_' + k] = (dsem[k], dcnt[k])
            elif signal[i]:
                ecnt[E] += 1
                inst.then_inc(esem[E], 1)
                ev[i] = (esem[E], ecnt[E], 's_' + E)
        if cur_scope:
            nc.leave_named_scope(cur_scope[0], cur_scope[1], False)
        for nm, (s, v) in final.items():
            if waited['sp'].get(nm, 0) < v:
                nc.sync.wait_ge(s, v)
        self.n_ops = len(ops)
        self.ecnt = ecnt
        self.dcnt = dcnt


class WStream:
    def __init__(self, ring, nslots, plan=None):
        self.ring = ring
        self.nslots = nslots
        self.plan = plan
        self.rec = []
        self.i = 0
        self.issued = 0

    def _issue(self, P, upto):
        while self.issued <= upto and self.issued < len(self.plan):
            j = self.issued
            parts, view, res = self.plan[j]
            s = j % self.nslots
            for src, sub in parts:
                dst = sub(view(self.ring, s))
                P.add('sp', R.dma_start(out=dst, in_=src), reads=[res], writes=['wr%d' % s], dma='wr%d' % s)
            self.issued += 1

    def get(self, P, src, view, res='win_b'):
        parts = src if isinstance(src, list) else [(src, lambda v: v)]
        j = self.i
        self.i += 1
        s = j % self.nslots
        if self.plan is None:
            self.rec.append((parts, view, res))
        else:
            self._issue(P, j + self.nslots - 1)
        return view(self.ring, s), 'wr%d' % s


def _view_k8(ring, s):
    return ring[:, s, 0:4096].rearrange("p (k c) -> p k c", k=8)


def _view_k8n(n):
    def f(ring, s):
        return ring[:, s, 0:8 * n].rearrange("p (k c) -> p k c", k=8)
    return f


def _view_dn(ring, s):
    return ring[:, s, 0:NF * 128].rearrange("p (k c) -> p k c", k=NF)


def build_program():
    nc = bass.Bass("TRN2", target_bir_lowering=False)

    def din(name, shape, dt=F32):
        return nc.dram_tensor(name, list(shape), dt, kind="ExternalInput").ap()

    x_d = din("x", [L, D])
    ctx_d = din("ctx", [LC, D])
    ccol_d = din("ccol", [128, 16])
    wmod_d = din("w_mod", [D, 6 * D])
    bmod_d = din("bmodc", [128, 48])
    n1g_d = din("n1g", [128, 8])
    n2g_d = din("n2g", [128, 8])
    win_d = din("w_in", [D, 3584])
    dlog_d = din("dlog", [1, 8])
    gng_d = din("gng", [1, 1024])
    dlam_d = din("dlam", [1, 256])
    subg_d = din("subg", [1, 512])
    wout_d = din("w_out", [D, D])
    wup_d = din("w_up", [D, 2 * DFF])
    convw_d = din("convwc", [128, 3 * 44])
    convb_d = din("convbc", [128, 44])
    wdn_d = din("w_down", [DFF, D])
    fing_d = din("fing", [1, D])
    ident_d = din("ident", [128, 128])
    cm_d = din("cmask", [128, 256])
    decexp_d = din("decexp", [128, 32])
    ropeR_d = din("rope_r", [L, 128])
    ropeD_d = din("rope_d", [L, 64])
    out_d = nc.dram_tensor("out", [L, D], F32, kind="ExternalOutput").ap()
    KTd = nc.dram_tensor("scr_kt", [4, 128, KT_ALL * 128], BF16).ap()
    Vd = nc.dram_tensor("scr_v", [4, 128, KT_ALL * 129], BF16).ap()
    Sbd = nc.dram_tensor("scr_sb", [32, 128, 512], BF16).ap()
    KRd = nc.dram_tensor("scr_kr", [8, 128, 2048], BF16).ap()
    RVd = nc.dram_tensor("scr_rv", [8, 128, 2048], BF16).ap()
    HTd = nc.dram_tensor("scr_ht", [8, 128, 4096], BF16).ap()
    winb_d = nc.dram_tensor("scr_win", [D, 3584], BF16).ap()
    woutb_d = nc.dram_tensor("scr_wout", [D, D], BF16).ap()
    wupb_d = nc.dram_tensor("scr_wup", [D, 2 * DFF], BF16).ap()
    wdnb_d = nc.dram_tensor("scr_wdn", [DFF, D], BF16).ap()
    dbg = {}

    def sb(name, shape, dt=F32):
        return nc.alloc_sbuf_tensor("s_" + name, list(shape), dt)

    identb = sb("identb", [128, 128], BF16)
    identf = sb("identf", [128, 128])
    cm = sb("cm", [128, 256])
    maskF = sb("maskF", [128, 4, 128])
    maskB = sb("maskB", [128, 4, 128])
    decexp = sb("decexp", [128, 32])
    dect = sb("dect", [128, 32])
    dbase = sb("dbase", [128, 8])
    gnT = sb("gnT", [128, 1024])
    subT = sb("subT", [128, 512])
    finT = sb("finT", [128, 1024])
    modc = sb("modc", [128, 48, 2])
    cols = sb("cols", [128, 8, 8])
    ccol = sb("ccol", [128, 16])
    scol = sb("scol", [128, 16])
    bmodc = sb("bmodc", [128, 48])
    n1g = sb("n1g", [128, 8])
    n2g = sb("n2g", [128, 8])
    convw = sb("convw", [128, 3, 44])
    convb = sb("convb", [128, 44])
    lamt = sb("lamt", [128, 256])
    lams = sb("lams", [128, 8])
    mhalf = sb("mhalf", [128, 16])
    stat = sb("stat", [128, 96])
    Sf = sb("Sf", [128, 512])
    Sfb = sb("Sfb", [128, 2, 512], BF16)
    Sbs = sb("Sbs", [128, 512])
    Sbb = sb("Sbb", [128, 4, 512], BF16)
    xblk = sb("xblk", [128, 2, 4, 1024])
    xst = sb("xst", [128, 1024])
    ybf = sb("ybf", [128, 4, 1024], BF16)
    hT = sb("hT", [128, 8, 512], BF16)
    h2T = sb("h2T", [128, 2, 8, 514], BF16)
    qr = sb("qr", [128, 4, 512], BF16)
    kr = sb("kr", [128, 4, 512], BF16)
    rv = sb("rv", [128, 4, 512], BF16)
    vpf = sb("vpf", [128, 4, 512], BF16)
    sg = sb("sg", [128, 4, 512])
    qT = sb("qT", [128, 2, 512], BF16)
    kTt = sb("kTt", [128, 2, 512], BF16)
    AfT = sb("AfT", [128, 2, 512], BF16)
    AbT = sb("AbT", [128, 2, 512], BF16)
    sbT = sb("sbT", [128, 4, 512], BF16)
    tmp = sb("tmp", [128, 6, 512])
    ropeR = sb("ropeR", [128, 4, 128])
    ropeD = sb("ropeD", [128, 4, 64])
    PT = sb("PT", [128, 4, 512], BF16)
    Vst = sb("Vst", [128, 4, 4, 129], BF16)
    ring = sb("ring", [128, 2, 4096], BF16)
    KVA = sb("KVA", [128, 17920], BF16)
    actT = KVA[:, 0:NF * 512].rearrange("p (f t) -> p f t", f=NF)

    def kvK(s):
        return KVA[:, s * 8960: s * 8960 + 4352]

    def kvV(s):
        return KVA[:, s * 8960 + 4352: s * 8960 + 4352 + KT_ALL * 129].rearrange("p (k e) -> p k e", k=KT_ALL)

    SC = nc.alloc_psum_tensor("SC", [128, 2048], F32)
    banks = [SC[:, i * 512:(i + 1) * 512] for i in range(4)] + [nc.alloc_psum_tensor("B%d" % i, [128, 512], F32)[:, :] for i in range(4, 7)]
    BT = nc.alloc_psum_tensor("BT", [128, 1024], BF16)

    wplan = {'plan': None}

    def gen(P, ws):
        P.alias['actT'] = ['kv0', 'kv1']
        SP, PE, ACT, DVE, POOL = 'sp', 'pe', 'act', 'dve', 'pool'

        def bc(ap, shape):
            return ap.broadcast_to(list(shape))

        ld_first = True
        for src_, dst_, nrow, key in ((win_d, winb_d, D, 'win_b'), (wout_d, woutb_d, D, 'wout_b'), (wup_d, wupb_d, D, 'wup_b'), (wdn_d, wdnb_d, DFF, 'wdn_b')):
            for r0 in range(0, nrow, 256):
                P.add(POOL, R.dma_start(out=dst_[r0:r0 + 256, :].rearrange("(p a) c -> p a c", p=128),
                                        in_=src_[r0:r0 + 256, :].rearrange("(p a) c -> p a c", p=128)), writes=[key], dma='wc_' + key)
        def ld(dst, src, name, eng=SP):
            P.add(eng, R.dma_start(out=dst, in_=src), writes=[name], dma=name)
        ld(identf[:], ident_d, 'identf')
        ld(identb[:], ident_d, 'identb', POOL)
        ld(cm[:], cm_d, 'cm')
        ld(decexp[:], decexp_d, 'decexp')
        ld(ccol[:], ccol_d, 'ccol')
        ld(bmodc[:], bmod_d, 'bmodc')
        ld(n1g[:], n1g_d, 'n1g')
        ld(n2g[:], n2g_d, 'n2g')
        ld(convw[:].rearrange("p a b -> p (a b)"), convw_d, 'convw')
        ld(convb[:], convb_d, 'convb')
        ld(dbase[:], bc(dlog_d, [128, 8]), 'dbase')
        ld(gnT[:], bc(gng_d, [128, 1024]), 'gnT')
        ld(subT[:], bc(subg_d, [128, 512]), 'subT')
        ld(finT[:], bc(fing_d, [128, 1024]), 'finT')
        ld(lamt[:], bc(dlam_d, [128, 256]), 'lamt')
        P.add(DVE, R.memset(mhalf[:], -0.5), writes=['mhalf'])
        P.add(DVE, R.memset(Vst[:], 1.0), writes=['Vst'])
        P.add(DVE, R.memset(Sf[:], 0.0), writes=['Sf'])
        P.add(DVE, R.memset(Sbs[:], 0.0), writes=['Sbs'])
        P.add(DVE, R.memset(Sfb[:], 0.0), writes=['Sfb0', 'Sfb1'])
        P.add(DVE, R.memset(Sbb[:], 0.0), writes=['Sbb0', 'Sbb1', 'Sbb2', 'Sbb3'])
        P.add(DVE, R.memset(h2T[:], 0.0), writes=['h2T0', 'h2T1'])
        P.add(POOL, R.tensor_scalar(out=gnT[:], in0=gnT[:], scalar1=0.5, scalar2=None, op0=ALU.mult),
              reads=['gnT'], writes=['gnT'])
        P.add(POOL, R.tensor_scalar(out=subT[:], in0=subT[:], scalar1=1.0 - LAMBDA_INIT, scalar2=None, op0=ALU.mult),
              reads=['subT'], writes=['subT'])
        P.add(ACT, R.activation(out=dbase[:], in_=dbase[:], func=AF.Exp, scale=-1.0), reads=['dbase'], writes=['dbase'])
        P.add(POOL, R.tensor_scalar(out=dbase[:], in0=dbase[:], scalar1=1.0, scalar2=None, op0=ALU.add),
              reads=['dbase'], writes=['dbase'])
        P.add(POOL, R.tensor_tensor(out=dect[:].rearrange("p (a b) -> p a b", a=4),
                                              in0=bc(dbase[:].unsqueeze(1), [128, 4, 8]),
                                              in1=decexp[:].rearrange("p (a b) -> p a b", a=4), op=ALU.pow),
              reads=['dbase', 'decexp'], writes=['dect'])
        P.add(POOL, R.tensor_scalar(out=dect[:, 0:8], in0=dect[:, 0:8], scalar1=128.0 ** -0.5, scalar2=None, op0=ALU.mult),
              reads=['dect'], writes=['dect'])
        P.add(POOL, R.tensor_scalar(out=dect[:, 16:24], in0=dect[:, 16:24], scalar1=128.0 ** -0.5, scalar2=None, op0=ALU.mult),
              reads=['dect'], writes=['dect'])
        for h in range(4):
            P.add(DVE, R.tensor_scalar(out=maskF[:, h, :], in0=cm[:, 0:128], scalar1=dect[:, h:h + 1], scalar2=None, op0=ALU.mult),
                  reads=['cm', 'dect'], writes=['maskF'])
            P.add(DVE, R.tensor_scalar(out=maskB[:, h, :], in0=cm[:, 128:256], scalar1=dect[:, 4 + h:5 + h], scalar2=None, op0=ALU.mult),
                  reads=['cm', 'dect'], writes=['maskB'])
        P.add(DVE, R.tensor_tensor(out=lamt[:, 0:64], in0=lamt[:, 0:64], in1=lamt[:, 64:128], op=ALU.mult), reads=['lamt'], writes=['lamt'])
        P.add(DVE, R.tensor_tensor(out=lamt[:, 128:192], in0=lamt[:, 128:192], in1=lamt[:, 192:256], op=ALU.mult), reads=['lamt'], writes=['lamt'])
        P.add(DVE, R.tensor_reduce(out=lams[:, 0:1], in_=lamt[:, 0:64], axis=AX.X, op=ALU.add), reads=['lamt'], writes=['lams'])
        P.add(DVE, R.tensor_reduce(out=lams[:, 1:2], in_=lamt[:, 128:192], axis=AX.X, op=ALU.add), reads=['lamt'], writes=['lams'])
        P.add(ACT, R.activation(out=lams[:, 2:4], in_=lams[:, 0:2], func=AF.Exp), reads=['lams'], writes=['lams'])
        P.add(DVE, R.tensor_tensor(out=lams[:, 4:5], in0=lams[:, 3:4], in1=lams[:, 2:3], op=ALU.subtract), reads=['lams'], writes=['lams'])
        P.add(DVE, R.tensor_scalar(out=lams[:, 5:6], in0=lams[:, 4:5], scalar1=-LAMBDA_INIT, scalar2=None, op0=ALU.add), reads=['lams'], writes=['lams'])
        neglam = lams[:, 5:6]

        P.add(ACT, R.activation(out=scol[:], in_=ccol[:], func=AF.Tanh, scale=0.5), reads=['ccol'], writes=['scol'])
        P.add(DVE, R.scalar_tensor_tensor(out=scol[:], in0=scol[:], scalar=1.0, in1=ccol[:], op0=ALU.add, op1=ALU.mult),
              reads=['scol', 'ccol'], writes=['scol'])
        P.add(DVE, R.tensor_scalar(out=scol[:], in0=scol[:], scalar1=0.5, scalar2=None, op0=ALU.mult), reads=['scol'], writes=['scol'])
        scv = scol[:].rearrange("p (a k) -> p k a", a=2)
        wm_v = wmod_d.rearrange("(k p) c -> p k c", p=128)
        def mod_load(g, stage, rn, key):
            P.add(SP, R.dma_start(out=stage, in_=wm_v[:, :, g * 512:(g + 1) * 512]), writes=[rn], dma=key)

        def mod_mm(g, stage, rn, bank, bname, col0):
            for ct in range(4):
                j = g * 4 + ct
                for k in range(8):
                    P.add(PE, R.matmul(out=bank[:, 2 * j - col0:2 * j - col0 + 2], lhsT=stage[:, k, ct * 128:(ct + 1) * 128], rhs=scv[:, k, :],
                                       start=(k == 0), stop=(k == 7)), reads=[rn, 'scol'], writes=[bname])

        for g in range(4):
            s = g % 2
            stage = xblk[:, s, :, :].rearrange("p a b -> p (a b)").rearrange("p (k c) -> p k c", k=8)
            mod_load(g, stage, 'xblk%d' % s, 'xblk%d' % s)
            mod_mm(g, stage, 'xblk%d' % s, banks[0], 'B0', 0)
        P.add(DVE, R.tensor_tensor(out=modc[:, 0:16, :], in0=banks[0][:, 0:32].rearrange("p (j a) -> p j a", a=2),
                                   in1=bc(bmodc[:, 0:16].unsqueeze(2), [128, 16, 2]), op=ALU.add),
              reads=['B0', 'bmodc'], writes=['modc'])
        def mcol(which, a):
            return modc[:, which * 8:(which + 1) * 8, a]
        for a, (ia, ish) in enumerate([(0, 2), (1, 3)]):
            P.add(DVE, R.scalar_tensor_tensor(out=cols[:, ia, :], in0=mcol(1, a), scalar=1.0, in1=n1g[:], op0=ALU.add, op1=ALU.mult),
                  reads=['modc', 'n1g'], writes=['cols'])
            P.add(DVE, R.tensor_copy(out=cols[:, ish, :], in_=mcol(0, a)), reads=['modc'], writes=['cols'])

        def mod_stage2(g):
            s_ = g % 2
            return KVA[:, s_ * 8192:(s_ + 1) * 8192].bitcast(F32).rearrange("p (k c) -> p k c", k=8), 'kv%d' % s_, 'modst%d' % s_

        def mod_part2_step(i):
            if i == 0:
                st_, rn_, key_ = mod_stage2(4)
                mod_load(4, st_, rn_, key_)
            if 4 + i + 1 < 12:
                st_, rn_, key_ = mod_stage2(4 + i + 1)
                mod_load(4 + i + 1, st_, rn_, key_)
            if 4 + i < 12:
                st_, rn_, key_ = mod_stage2(4 + i)
                mod_mm(4 + i, st_, rn_, banks[6], 'B6', 32)

        def mod_finish():
            P.add(DVE, R.tensor_tensor(out=modc[:, 16:48, :], in0=banks[6][:, 0:64].rearrange("p (j a) -> p j a", a=2),
                                       in1=bc(bmodc[:, 16:48].unsqueeze(2), [128, 32, 2]), op=ALU.add),
                  reads=['B6', 'bmodc'], writes=['modc2'])
            P.add(DVE, R.scalar_tensor_tensor(out=cols[:, 4, :], in0=mcol(4, 0), scalar=1.0, in1=n2g[:], op0=ALU.add, op1=ALU.mult),
                  reads=['modc2', 'n2g'], writes=['cols2'])
            P.add(DVE, R.tensor_copy(out=cols[:, 5, :], in_=mcol(3, 0)), reads=['modc2'], writes=['cols2'])
            P.add(DVE, R.tensor_copy(out=cols[:, 6, :], in_=mcol(2, 0)), reads=['modc2'], writes=['cols2'])
            P.add(DVE, R.tensor_scalar(out=cols[:, 7, :], in0=mcol(5, 0), scalar1=0.5, scalar2=None, op0=ALU.mult), reads=['modc2'], writes=['cols2'])

        def ckpt(name):
            P.phase = name
            if STOP == name:
                raise _Stop()
        ckpt('mod')
        cnt = {'bt': 0, 'st': 0}

        def rsqrt_col(dst, src, scale, eps, n, rn='stat'):
            P.add(POOL, R.tensor_scalar(out=dst, in0=src, scalar1=scale, scalar2=eps, op0=ALU.mult, op1=ALU.add),
                  reads=[rn], writes=[rn])
            P.add(POOL, R.tensor_tensor(out=dst, in0=dst, in1=mhalf[:, 0:n], op=ALU.pow), reads=[rn, 'mhalf'], writes=[rn])

        def load_x(slot, src_ap, nt, extra=None):
            P.add(SP, R.dma_start(out=xblk[:, slot, 0:nt, :], in_=src_ap.rearrange("(t p) d -> p t d", p=128)),
                  writes=['xblk%d' % slot], dma='xblk%d' % slot)

        def norm_stats(slot, nt):
            xn = 'xblk%d' % slot
            for t in range(nt):
                P.add(ACT, R.activation(out=ybf[:, t, :], in_=xblk[:, slot, t, :], func=AF.Square,
                                        accum_out=stat[:, 8 + t:9 + t]), reads=[xn], writes=['ybf', 'stat'])
            rsqrt_col(stat[:, 8:8 + nt], stat[:, 8:8 + nt], 1.0 / D, EPS, nt)
            for t in range(nt):
                P.add(DVE, R.tensor_scalar(out=ybf[:, t, :], in0=xblk[:, slot, t, :], scalar1=stat[:, 8 + t:9 + t],
                                                                   scalar2=None, op0=ALU.mult), reads=[xn, 'stat'], writes=['ybf'])

        def norm_stream_load(src_ap, t):
            P.add(SP, R.dma_start(out=xst[:], in_=src_ap[t * 128:(t + 1) * 128, :]), writes=['xst'], dma='xst')

        def norm_stream_tile(t):
            P.add(ACT, R.activation(out=ybf[:, t, :], in_=xst[:], func=AF.Square, accum_out=stat[:, 12 + t:13 + t]),
                  reads=['xst'], writes=['ybf', 'statS%d' % t])
            rsqrt_col(stat[:, 12 + t:13 + t], stat[:, 12 + t:13 + t], 1.0 / D, EPS, 1, 'statS%d' % t)
            P.add(DVE, R.tensor_scalar(out=ybf[:, t, :], in0=xst[:], scalar1=stat[:, 12 + t:13 + t], scalar2=None, op0=ALU.mult),
                  reads=['xst', 'statS%d' % t], writes=['ybf'])

        def norm_stats_stream(src_ap, nt):
            for t in range(nt):
                norm_stream_load(src_ap, t)
                norm_stream_tile(t)

        def norm_tr(nt, dstT, dst_off, dst_name, acol, shcol):
            for k in range(8):
                for t in range(nt):
                    P.add(PE, R.transpose(out=BT[:, t * 128:(t + 1) * 128], in_=ybf[:, t, k * 128:(k + 1) * 128], identity=identb[:]),
                          reads=['ybf', 'identb'], writes=['BT'])
                P.add(ACT, R.activation(out=dstT(k)[:, dst_off:dst_off + nt * 128], in_=BT[:, 0:nt * 128], func=AF.Identity,
                                        bias=shcol[:, k:k + 1], scale=acol[:, k:k + 1]),
                      reads=['BT', 'cols', 'cols2'], writes=[dst_name])

        def norm_to_T(slot, nt, dstT, dst_off, dst_name, acol, shcol):
            norm_stats(slot, nt)
            norm_tr(nt, dstT, dst_off, dst_name, acol, shcol)

        def proj_tm(nt, g, evac, ws_):
            wv, wn = ws_.get(P, winb_d.rearrange("(k p) c -> p k c", p=128)[:, :, g * 512:(g + 1) * 512], _view_k8)
            for t in range(nt):
                bi = cnt['bt'] % 2
                cnt['bt'] += 1
                bk, bn = banks[bi], 'B%d' % bi
                for k in range(8):
                    P.add(PE, R.matmul(out=bk[:, :], lhsT=hT[:, k, t * 128:(t + 1) * 128], rhs=wv[:, k, :],
                                                                  start=(k == 0), stop=(k == 7)), reads=['hT', wn], writes=[bn])
                evac(t, bk, bn)

        def rope_evac(dst, table, nh, half, name, tname):
            def f(t, bk, bn):
                v = bk[:, :].rearrange("p (h a d) -> p h a d", h=nh, a=2)
                x1, x2 = v[:, :, 0, :], v[:, :, 1, :]
                cs = bc(table[:, t, 0:half].unsqueeze(1), [128, nh, half])
                sn = bc(table[:, t, half:2 * half].unsqueeze(1), [128, nh, half])
                o = dst[:, t, :].rearrange("p (h a d) -> p h a d", h=nh, a=2)
                tv = [tmp[:, i, 0:256].rearrange("p (h d) -> p h d", h=nh) for i in range(4)]
                P.add(DVE, R.tensor_tensor(out=tv[0], in0=x1, in1=cs, op=ALU.mult), reads=[bn, tname], writes=['tmp0'])
                P.add(DVE, R.tensor_tensor(out=tv[1], in0=x2, in1=sn, op=ALU.mult), reads=[bn, tname], writes=['tmp1'])
                P.add(DVE, R.tensor_tensor(out=tv[2], in0=x1, in1=sn, op=ALU.mult), reads=[bn, tname], writes=['tmp2'])
                P.add(DVE, R.tensor_tensor(out=tv[3], in0=x2, in1=cs, op=ALU.mult), reads=[bn, tname], writes=['tmp3'])
                P.add(DVE, R.tensor_tensor(out=o[:, :, 0, :], in0=tv[0], in1=tv[1], op=ALU.subtract), reads=['tmp0', 'tmp1'], writes=[name])
                P.add(DVE, R.tensor_tensor(out=o[:, :, 1, :], in0=tv[2], in1=tv[3], op=ALU.add), reads=['tmp2', 'tmp3'], writes=[name])
            return f

        def copy_evac(dst, name):
            def f(t, bk, bn):
                P.add(ACT, R.copy(out=dst[:, t, :], in_=bk[:, :]), reads=[bn], writes=[name])
            return f

        def vscale_evac(dsts):
            def f(t, bk, bn):
                for dst, name, co in dsts:
                    if co is None:
                        P.add(ACT, R.copy(out=dst[:, t, :], in_=bk[:, :]), reads=[bn], writes=[name])
                    else:
                        P.add(DVE, R.tensor_tensor(
                            out=dst[:, t, :].rearrange("p (h d) -> p h d", h=4), in0=bk[:, :].rearrange("p (h d) -> p h d", h=4),
                            in1=bc(dect[:, co:co + 4].unsqueeze(2), [128, 4, 128]), op=ALU.mult), reads=[bn, 'dect'], writes=[name])
            return f

        def state_update(ksrc, vsrc, t, S, cdec_off, sname, deps):
            for h in range(4):
                P.add(PE, R.matmul(out=banks[5][:, h * 128:(h + 1) * 128], lhsT=ksrc[:, t, h * 128:(h + 1) * 128],
                                                  rhs=vsrc[:, t, h * 128:(h + 1) * 128], start=True, stop=True), reads=deps, writes=['B5'])
            for h in range(4):
                P.add(DVE, R.scalar_tensor_tensor(out=S[:, h * 128:(h + 1) * 128], in0=S[:, h * 128:(h + 1) * 128],
                                                                 scalar=dect[:, cdec_off + h:cdec_off + h + 1], in1=banks[5][:, h * 128:(h + 1) * 128],
                                                                 op0=ALU.mult, op1=ALU.add), reads=['B5', sname, 'dect'], writes=[sname])

        def pass1_block(src_ap, nt, is_ctx, tile0, kt0, slot, nxt=None):
            nm_a, nm_s = (cols[:, 1, :], cols[:, 3, :]) if is_ctx else (cols[:, 0, :], cols[:, 2, :])
            norm_tr(nt, lambda k: hT[:, k, :], 0, 'hT', nm_a, nm_s)
            if not is_ctx:
                P.add(SP, R.dma_start(out=HTd[tile0 // 4], in_=hT[:].rearrange("p k t -> p (k t)")), reads=['hT'], writes=['HTd%d' % (tile0 // 4)], dma='st_ht')
            if nxt is not None:
                nxt()
            if not is_ctx:
                t0 = tile0 * 128
                P.add(SP, R.dma_start(out=ropeR[:], in_=ropeR_d[t0:t0 + 512, :].rearrange("(t p) d -> p t d", p=128)), writes=['ropeR'], dma='ropeR')
                P.add(SP, R.dma_start(out=ropeD[:], in_=ropeD_d[t0:t0 + 512, :].rearrange("(t p) d -> p t d", p=128)), writes=['ropeD'], dma='ropeD')
            proj_tm(nt, 1, copy_evac(kr, 'kr') if is_ctx else rope_evac(kr, ropeR, 4, 64, 'kr', 'ropeR'), ws)
            if not is_ctx:
                P.add(SP, R.dma_start(out=KRd[tile0 // 4], in_=kr[:].rearrange("p t c -> p (t c)")), reads=['kr'], writes=['KRd%d' % (tile0 // 4)], dma='st_kr')
            if is_ctx:
                proj_tm(nt, 2, vscale_evac([(vpf, 'vpf', 16), (rv, 'rv', 20)]), ws)
            else:
                proj_tm(nt, 2, vscale_evac([(vpf, 'vpf', None), (rv, 'rv', 20)]), ws)
                P.add(SP, R.dma_start(out=RVd[tile0 // 4], in_=vpf[:].rearrange("p t c -> p (t c)")), reads=['vpf'], writes=['RVd%d' % (tile0 // 4)], dma='st_rv')
            proj_tm(nt, 5, copy_evac(sbT, 'sbT') if is_ctx else rope_evac(sbT, ropeD, 8, 32, 'sbT', 'ropeD'), ws)
            for h in range(4):
                for t in range(nt):
                    P.add(PE, R.transpose(out=BT[:, t * 128:(t + 1) * 128], in_=sbT[:, t, h * 128:(h + 1) * 128], identity=identb[:]),
                          reads=['sbT', 'identb'], writes=['BT'])
                P.add(ACT if h % 2 else DVE, (R.copy(out=qr[:, h, 0:nt * 128], in_=BT[:, 0:nt * 128])) if h % 2 else
                      (R.tensor_copy(out=qr[:, h, 0:nt * 128], in_=BT[:, 0:nt * 128])), reads=['BT'], writes=['qr'])
            P.add(SP, R.dma_start(out=KTd[:, :, kt0 * 128: kt0 * 128 + nt * 128].rearrange("h p t -> p h t"), in_=qr[:, :, 0:nt * 128]),
                  reads=['qr'], writes=['KTd'], dma='st_kt')
            def v_evac(t, bk, bn):
                P.add(ACT, R.copy(out=Vst[:, t, :, 0:128], in_=bk[:, :].rearrange("p (h d) -> p h d", h=4)), reads=[bn], writes=['Vst'])
            proj_tm(nt, 6, v_evac, ws)
            for h in range(4):
                P.add(SP, R.dma_start(out=Vd[h][:, kt0 * 129:(kt0 + nt) * 129].rearrange("p (t e) -> p t e", e=129), in_=Vst[:, 0:nt, h, :]),
                      reads=['Vst'], writes=['Vd'], dma='st_v')
            for t in reversed(range(nt)):
                if not is_ctx:
                    gt = tile0 + t
                    s4 = cnt['st'] % 4
                    cnt['st'] += 1
                    P.add(ACT, R.copy(out=Sbb[:, s4, :], in_=Sbs[:]), reads=['Sbs'], writes=['Sbb%d' % s4])
                    P.add(SP, R.dma_start(out=Sbd[gt], in_=Sbb[:, s4, :]), reads=['Sbb%d' % s4], writes=['Sbd%d' % gt], dma='st_sb%d' % s4)
                state_update(kr, rv, t, Sbs, 28, 'Sbs', ['kr', 'rv'])
            if is_ctx:
                for t in range(nt):
                    state_update(kr, vpf, t, Sf, 24, 'Sf', ['kr', 'vpf'])

        p1 = [(ctx_d, 2, True, 0, 0)] + [(x_d[n * 512:(n + 1) * 512, :], 4, False, n * 4, 2 + n * 4) for n in reversed(range(NB))]
        load_x(0, p1[0][0], p1[0][1])
        load_x(1, p1[1][0], p1[1][1])
        norm_stats(0, p1[0][1])
        for i, (src_, nt_, isc_, tile0_, kt0_) in enumerate(p1):
            def nxt(i=i):
                if i + 1 < len(p1):
                    norm_stats((i + 1) % 2, p1[i + 1][1])
                if i + 2 < len(p1):
                    load_x(i % 2, p1[i + 2][0], p1[i + 2][1])
            mod_part2_step(i)
            pass1_block(src_, nt_, isc_, tile0_, kt0_, i % 2, nxt)
            ckpt('p1ctx' if i == 0 else 'p1b%d' % (NB - i))
        P.add(ACT, R.copy(out=Sfb[:, 0, :], in_=Sf[:]), reads=['Sf'], writes=['Sfb0'])

        def fm_proj_residual(slot, wsrc_fn, nk, rhs_fn, rhs_names, gcol, wview, ws_):
            xn = 'xblk%d' % slot
            for dt in range(8):
                wv, wn, col0 = wsrc_fn(dt)
                bi = dt % 2
                bk, bn = banks[bi], 'B%d' % bi
                for k in range(nk):
                    P.add(PE, R.matmul(out=bk[:, :], lhsT=wv[:, k, col0:col0 + 128], rhs=rhs_fn(k),
                                                                                start=(k == 0), stop=(k == nk - 1)), reads=[wn] + rhs_names, writes=[bn])
                ti = dt % 2
                P.add(ACT, R.activation(out=tmp[:, ti, :], in_=bk[:, :], func=AF.Identity, scale=gcol[:, dt:dt + 1]),
                      reads=[bn, 'cols', 'cols2'], writes=['tmp%d' % ti])
                b2, b2n = banks[2 + ti], 'B%d' % (2 + ti)
                for t in range(4):
                    P.add(PE, R.transpose(out=b2[:, t * 128:(t + 1) * 128], in_=tmp[:, ti, t * 128:(t + 1) * 128], identity=identf[:]),
                          reads=['tmp%d' % ti, 'identf'], writes=[b2n])
                P.add(DVE, R.tensor_tensor(out=xblk[:, slot, :, dt * 128:(dt + 1) * 128], in0=xblk[:, slot, :, dt * 128:(dt + 1) * 128],
                                                                   in1=b2[:, :].rearrange("p (t d) -> p t d", t=4), op=ALU.add), reads=[b2n, xn], writes=[xn])

        def ffn(n, ws_, hook=None):
            slot = n % 2
            hs = n % 2
            h2 = lambda k: h2T[:, hs, k, :]
            wupv = wupb_d.rearrange("(k p) c -> p k c", p=128)
            for fp in range(NF):
                if hook is not None:
                    hook(fp)
                par = fp % 2
                T = [tmp[:, 3 * par + i_, :] for i_ in range(3)]
                Tn = ['tmp%d' % (3 * par + i_) for i_ in range(3)]
                key = (n, fp // 2)
                if key not in ffn_w:
                    c0 = (fp // 2) * 256
                    ffn_w[key] = ws_.get(P, [(wupv[:, :, c0:c0 + 256], lambda v: v[:, :, 0:256]),
                                             (wupv[:, :, DFF + c0:DFF + c0 + 256], lambda v: v[:, :, 256:512])], _view_k8, res='wup_b')
                wv, wn = ffn_w[key]
                hb, hbn = banks[4 + par], 'B%d' % (4 + par)
                for ab in range(2):
                    ci = ab * 2 + (fp % 2)
                    bi = 2 * par + ab
                    bk, bn = banks[bi], 'B%d' % bi
                    hc = ab * 2
                    for k in range(8):
                        P.add(PE, R.matmul(out=bk[:, :], lhsT=wv[:, k, ci * 128:(ci + 1) * 128], rhs=h2(k)[:, 1:513],
                                           start=(k == 0), stop=(k == 7)), reads=[wn, 'h2T%d' % hs], writes=[bn])
                    for k in range(8):
                        P.add(PE, R.matmul(out=hb[:, hc:hc + 2], lhsT=wv[:, k, ci * 128:(ci + 1) * 128],
                                           rhs=h2(k)[:, 0:514:513], start=(k == 0), stop=(k == 7)),
                              reads=[wn, 'h2T%d' % hs], writes=[hbn])
                for ab in range(2):
                    f = fp + ab * NF
                    bi = 2 * par + ab
                    bk, bn = banks[bi], 'B%d' % bi
                    P.add(ACT, R.activation(out=T[ab], in_=bk[:, :], func=AF.Identity, bias=convb[:, f:f + 1], scale=convw[:, 1, f:f + 1]),
                          reads=[bn, 'convw', 'convb'], writes=[Tn[ab]])
                for ab in range(2):
                    f = fp + ab * NF
                    bi = 2 * par + ab
                    bk, bn = banks[bi], 'B%d' % bi
                    hc = ab * 2
                    tt, tn = T[ab], Tn[ab]
                    w0, w2 = convw[:, 0, f:f + 1], convw[:, 2, f:f + 1]
                    P.add(DVE, R.scalar_tensor_tensor(out=tt[:, 1:512], in0=bk[:, 0:511], scalar=w0, in1=tt[:, 1:512],
                                                      op0=ALU.mult, op1=ALU.add), reads=[bn, tn, 'convw'], writes=[tn])
                    P.add(DVE, R.scalar_tensor_tensor(out=tt[:, 0:511], in0=bk[:, 1:512], scalar=w2, in1=tt[:, 0:511],
                                                      op0=ALU.mult, op1=ALU.add), reads=[bn, tn, 'convw'], writes=[tn])
                    P.add(DVE, R.scalar_tensor_tensor(out=tt[:, 0:1], in0=hb[:, hc:hc + 1], scalar=w0, in1=tt[:, 0:1],
                                                      op0=ALU.mult, op1=ALU.add), reads=[hbn, tn, 'convw'], writes=[tn])
                    P.add(DVE, R.scalar_tensor_tensor(out=tt[:, 511:512], in0=hb[:, hc + 1:hc + 2], scalar=w2, in1=tt[:, 511:512],
                                                      op0=ALU.mult, op1=ALU.add), reads=[hbn, tn, 'convw'], writes=[tn])
                P.add(ACT, R.activation(out=T[2], in_=T[0], func=AF.Tanh, scale=0.5), reads=[Tn[0]], writes=[Tn[2]])
                P.add(POOL, R.tensor_tensor(out=T[2], in0=T[2], in1=T[0], op=ALU.mult), reads=[Tn[2], Tn[0]], writes=[Tn[2]])
                P.add(POOL, R.tensor_tensor(out=T[2], in0=T[2], in1=T[0], op=ALU.add), reads=[Tn[2], Tn[0]], writes=[Tn[2]])
                P.add(POOL, R.tensor_tensor(out=actT[:, fp, :], in0=T[2], in1=T[1], op=ALU.mult),
                      reads=[Tn[2], Tn[1]], writes=['actT'])
            wdv = wdnb_d.rearrange("(k p) c -> p k c", p=128)

            def wsrc(dt):
                wv, wn = ws_.get(P, wdv[:, :, dt * 128:(dt + 1) * 128], _view_dn, res='wdn_b')
                return wv, wn, 0
            fm_proj_residual(slot, wsrc, NF, lambda k: actT[:, k, :], ['actT'], cols[:, 7, :], _view_dn, ws_)
            xn = 'xblk%d' % slot
            for t in range(4):
                P.add(ACT, R.activation(out=tmp[:, 4, 0:512], in_=xblk[:, slot, t, 0:512], func=AF.Square,
                                                       accum_out=stat[:, 16 + 2 * t:17 + 2 * t]), reads=[xn], writes=['tmp4', 'stat'])
                P.add(ACT, R.activation(out=tmp[:, 4, 0:512], in_=xblk[:, slot, t, 512:1024], func=AF.Square,
                                                       accum_out=stat[:, 17 + 2 * t:18 + 2 * t]), reads=[xn], writes=['tmp4', 'stat'])
            P.add(DVE, R.tensor_reduce(out=stat[:, 24:28], in_=stat[:, 16:24].rearrange("p (t a) -> p t a", a=2), axis=AX.X, op=ALU.add),
                  reads=['stat'], writes=['stat'])
            rsqrt_col(stat[:, 24:28], stat[:, 24:28], 1.0 / D, EPS, 4)
            for t in range(4):
                P.add(DVE, R.scalar_tensor_tensor(out=xblk[:, slot, t, :], in0=xblk[:, slot, t, :], scalar=stat[:, 24 + t:25 + t],
                                                                                      in1=finT[:], op0=ALU.mult, op1=ALU.mult), reads=[xn, 'stat', 'finT'], writes=[xn])
            P.add(SP, R.dma_start(out=out_d[n * 512:(n + 1) * 512, :].rearrange("(t p) d -> p t d", p=128), in_=xblk[:, slot, :, :]),
                  reads=[xn], writes=['out'], dma='st_out%d' % slot)

        ffn_w = {}

        def load_kv(n, h):
            s = (n * 4 + h) % 2
            P.add(SP, R.dma_start(out=kvK(s), in_=KTd[h]), reads=['KTd'], writes=['kv%d' % s], dma='kvk%d' % s)
            P.add(SP, R.dma_start(out=kvV(s), in_=Vd[h].rearrange("p (k e) -> p k e", e=129)), reads=['Vd'], writes=['kv%d' % s], dma='kvv%d' % s)

        def prefetch(n):
            t0 = n * 512
            P.add(SP, R.dma_start(out=ropeR[:], in_=ropeR_d[t0:t0 + 512, :].rearrange("(t p) d -> p t d", p=128)), writes=['ropeR'], dma='ropeR')
            P.add(SP, R.dma_start(out=ropeD[:], in_=ropeD_d[t0:t0 + 512, :].rearrange("(t p) d -> p t d", p=128)), writes=['ropeD'], dma='ropeD')
            P.add(SP, R.dma_start(out=sbT[:], in_=Sbd[n * 4:(n + 1) * 4].rearrange("t p c -> p t c")), reads=['Sbd%d' % (n * 4 + i_) for i_ in range(4)], writes=['sbT'], dma='sbT')
            P.add(SP, R.dma_start(out=kr[:].rearrange("p t c -> p (t c)"), in_=KRd[n]), reads=['KRd%d' % n], writes=['kr'], dma='ld_kr')
            P.add(SP, R.dma_start(out=rv[:].rearrange("p t c -> p (t c)"), in_=RVd[n]), reads=['RVd%d' % n], writes=['rv'], dma='ld_rv')
            P.add(SP, R.dma_start(out=hT[:].rearrange("p k t -> p (k t)"), in_=HTd[n]), reads=['HTd%d' % n], writes=['hT'], dma='ld_ht')

        def attn_block(n):
            slot = n % 2
            xn = 'xblk%d' % slot
            t0 = n * 512
            load_x(slot, x_d[t0:t0 + 512, :], 4)
            ckpt('a%dn' % n)
            proj_tm(4, 0, rope_evac(qr, ropeR, 4, 64, 'qr', 'ropeR'), ws)
            ckpt('a%dp0' % n)
            ckpt('a%dp1' % n)
            for t_ in range(4):
                P.add(DVE, R.tensor_tensor(out=vpf[:, t_, :].rearrange("p (h d) -> p h d", h=4), in0=rv[:, t_, :].rearrange("p (h d) -> p h d", h=4),
                                           in1=bc(dect[:, 16:20].unsqueeze(2), [128, 4, 128]), op=ALU.mult), reads=['rv', 'dect'], writes=['vpf'])
            ckpt('a%dp2' % n)

            def gate_evac(t, bk, bn):
                P.add(ACT, R.activation(out=tmp[:, 4, :], in_=bk[:, :], func=AF.Tanh, scale=0.5), reads=[bn], writes=['tmp4'])
                P.add(DVE, R.scalar_tensor_tensor(out=sg[:, t, :], in0=tmp[:, 4, :], scalar=1.0, in1=bk[:, :], op0=ALU.add, op1=ALU.mult),
                      reads=[bn, 'tmp4'], writes=['sg'])
            proj_tm(4, 3, gate_evac, ws)
            dqb = Vst[:].rearrange("p a b c -> p (a b c)")[:, 0:2048].rearrange("p (t c) -> p t c", t=4)
            proj_tm(4, 4, rope_evac(dqb, ropeD, 8, 32, 'Vst', 'ropeD'), ws)
            ckpt('a%dp' % n)
            load_kv(n, 0)
            def ret_A(t):
                par = t % 2
                for src, dst, nm in ((qr, qT, 'qT%d' % par), (kr, kTt, 'kT%d' % par)):
                    for h in range(4):
                        P.add(PE, R.transpose(out=BT[:, h * 128:(h + 1) * 128], in_=src[:, t, h * 128:(h + 1) * 128], identity=identb[:]),
                              reads=['qr' if src is qr else 'kr', 'identb'], writes=['BT'])
                    if src is qr:
                        P.add(ACT, R.copy(out=dst[:, par, :], in_=BT[:, 0:512]), reads=['BT'], writes=[nm])
                    else:
                        P.add(DVE, R.tensor_copy(out=dst[:, par, :], in_=BT[:, 0:512]), reads=['BT'], writes=[nm])
                for h in range(4):
                    P.add(PE, R.matmul(out=banks[4][:, h * 128:(h + 1) * 128], lhsT=kTt[:, par, h * 128:(h + 1) * 128],
                                       rhs=qT[:, par, h * 128:(h + 1) * 128], start=True, stop=True),
                          reads=['qT%d' % par, 'kT%d' % par], writes=['B4'])
                P.add(DVE, R.tensor_tensor(out=AfT[:, par, :], in0=banks[4][:, :], in1=maskF[:].rearrange("p h i -> p (h i)"), op=ALU.mult),
                      reads=['B4', 'maskF'], writes=['AfT%d' % par])
                P.add(DVE, R.tensor_tensor(out=AbT[:, par, :], in0=banks[4][:, :], in1=maskB[:].rearrange("p h i -> p (h i)"), op=ALU.mult),
                      reads=['B4', 'maskB'], writes=['AbT%d' % par])
                fs = (n * 4 + t) % 2
                for (A, an, bki, Ssrc, sn) in ((AfT, 'AfT%d' % par, 2 * par, Sfb[:, fs, :], 'Sfb%d' % fs), (AbT, 'AbT%d' % par, 2 * par + 1, sbT[:, t, :], 'sbT')):
                    for h in range(4):
                        P.add(PE, R.matmul(out=banks[bki][:, h * 128:(h + 1) * 128], lhsT=A[:, par, h * 128:(h + 1) * 128],
                                           rhs=rv[:, t, h * 128:(h + 1) * 128], start=True, stop=False),
                              reads=[an, 'rv'], writes=['B%d' % bki])
                        P.add(PE, R.matmul(out=banks[bki][:, h * 128:(h + 1) * 128], lhsT=qT[:, par, h * 128:(h + 1) * 128],
                                           rhs=Ssrc[:, h * 128:(h + 1) * 128], start=False, stop=True),
                              reads=['qT%d' % par, sn], writes=['B%d' % bki])
                state_update(kr, vpf, t, Sf, 24, 'Sf', ['kr', 'vpf'])
                ns = (n * 4 + t + 1) % 2
                P.add(ACT, R.copy(out=Sfb[:, ns, :], in_=Sf[:]), reads=['Sf'], writes=['Sfb%d' % ns])

            def ret_B(t):
                par = t % 2
                O = tmp[:, 2 * par:2 * par + 2, :].rearrange("p a b -> p (a b)")
                On = ['tmp%d' % (2 * par), 'tmp%d' % (2 * par + 1)]
                O4 = O.rearrange("p (g e) -> p g e", e=128)
                SQ = tmp[:, 4:6, :].rearrange("p a b -> p (a b)")
                SQ4 = SQ.rearrange("p (g e) -> p g e", e=128)
                st = stat[:, 32 + 24 * par:56 + 24 * par]
                sn = 'statR%d' % par
                P.add(DVE, R.tensor_tensor(out=O4, in0=SC[:, par * 1024:(par + 1) * 1024].rearrange("p (g e) -> p g e", e=128),
                                           in1=bc(dect[:, 8:16].unsqueeze(2), [128, 8, 128]), op=ALU.mult),
                      reads=['B%d' % (2 * par), 'B%d' % (2 * par + 1), 'dect'], writes=On)
                P.add(DVE, R.tensor_reduce(out=st[:, 0:8], in_=O4, axis=AX.X, op=ALU.add), reads=On, writes=[sn])
                P.add(POOL, R.tensor_tensor(out=SQ, in0=O, in1=O, op=ALU.mult), reads=On, writes=['tmp4', 'tmp5'])
                P.add(DVE, R.tensor_reduce(out=st[:, 8:16], in_=SQ4, axis=AX.X, op=ALU.add), reads=['tmp4', 'tmp5'], writes=[sn, sn + 'q'])
                P.add(POOL, R.tensor_scalar(out=st[:, 0:8], in0=st[:, 0:8], scalar1=1.0 / 128, scalar2=None, op0=ALU.mult), reads=[sn], writes=[sn])
                P.add(POOL, R.tensor_tensor(out=st[:, 16:24], in0=st[:, 0:8], in1=st[:, 0:8], op=ALU.mult), reads=[sn], writes=[sn])
                P.add(DVE, R.scalar_tensor_tensor(out=st[:, 8:16], in0=st[:, 8:16], scalar=1.0 / 128, in1=st[:, 16:24], op0=ALU.mult, op1=ALU.subtract),
                      reads=[sn, sn + 'q'], writes=[sn, sn + 'q'])
                rsqrt_col(st[:, 8:16], st[:, 8:16], 1.0, GN_EPS, 8, sn)
                P.add(DVE, R.scalar_tensor_tensor(out=st[:, 16:24], in0=st[:, 0:8], scalar=-1.0, in1=st[:, 8:16], op0=ALU.mult, op1=ALU.mult),
                      reads=[sn], writes=[sn])
                for d_, eng_ in ((0, DVE), (1, POOL)):
                    Oh = O4[:, 4 * d_:4 * d_ + 4, :]
                    P.add(eng_, R.tensor_tensor(out=Oh, in0=Oh, in1=bc(st[:, 8 + 4 * d_:12 + 4 * d_].unsqueeze(2), [128, 4, 128]), op=ALU.mult),
                          reads=[On[d_], sn], writes=[On[d_]])
                    P.add(eng_, R.tensor_tensor(out=Oh, in0=Oh, in1=bc(st[:, 16 + 4 * d_:20 + 4 * d_].unsqueeze(2), [128, 4, 128]), op=ALU.add),
                          reads=[On[d_], sn], writes=[On[d_]])
                    P.add(eng_, R.tensor_tensor(out=O[:, 512 * d_:512 * d_ + 512], in0=O[:, 512 * d_:512 * d_ + 512], in1=gnT[:, 512 * d_:512 * d_ + 512], op=ALU.mult),
                          reads=[On[d_], 'gnT'], writes=[On[d_]])
                P.add(DVE, R.tensor_tensor(out=O[:, 0:512], in0=O[:, 0:512], in1=O[:, 512:1024], op=ALU.add), reads=On, writes=[On[0]])
                P.add(DVE, R.tensor_tensor(out=ybf[:, t, 0:512], in0=O[:, 0:512], in1=sg[:, t, :], op=ALU.mult), reads=[On[0], 'sg'], writes=['ybf'])

            def dq_transposes():
                for h in range(4):
                    for t in range(4):
                        P.add(PE, R.transpose(out=BT[:, t * 128:(t + 1) * 128], in_=dqb[:, t, h * 128:(h + 1) * 128], identity=identb[:]),
                              reads=['Vst', 'identb'], writes=['BT'])
                    P.add(ACT, R.copy(out=kr[:, h, :], in_=BT[:, 0:512]), reads=['BT'], writes=['kr'])
            ret_A(0)
            for t in range(4):
                if t + 1 < 4:
                    ret_A(t + 1)
                if t == 2:
                    dq_transposes()
                ret_B(t)
            ckpt('a%dr' % n)
            ckpt('a%dd' % n)
            def emit_qk(h, kt):
                s_ = (n * 4 + h) % 2
                Kh_ = kvK(s_)
                pb = (kt % 2) * 2
                for m in range(2):
                    P.add(PE, R.matmul(out=banks[pb + m][:, :], lhsT=Kh_[m * 64:(m + 1) * 64, kt * 128:(kt + 1) * 128],
                                       rhs=kr[m * 64:(m + 1) * 64, h, :], start=True, stop=True),
                          reads=['kv%d' % s_, 'kr'], writes=['B%d' % (pb + m)])

            accs = tmp[:, 0:3, :].rearrange("p a b -> p (a b)")[:, 0:8 * 129].rearrange("p (a e) -> p a e", e=129)
            accn = ['tmp0', 'tmp1', 'tmp2']
            emit_qk(0, 0)
            for h in range(4):
                s = (n * 4 + h) % 2
                if h < 3:
                    load_kv(n, h + 1)
                Vh = kvV(s)
                kvn = 'kv%d' % s
                for kt in range(KT_ALL):
                    pb = (kt % 2) * 2
                    P.add(ACT, R.activation(out=PT[:, pb:pb + 2, :].rearrange("p a b -> p (a b)"), in_=SC[:, pb * 512:(pb + 2) * 512],
                                            func=AF.Exp, scale=0.125),
                          reads=['B%d' % pb, 'B%d' % (pb + 1)], writes=['PT%d' % pb, 'PT%d' % (pb + 1)])
                    if kt + 1 < KT_ALL:
                        emit_qk(h, kt + 1)
                    elif h < 3:
                        emit_qk(h + 1, 0)
                    for qt in range(4):
                        for m in range(2):
                            a = qt * 2 + m
                            bki, co = 4 + a // 3, (a % 3) * 129
                            P.add(PE, R.matmul(
                                out=banks[bki][:, co:co + 129], lhsT=PT[:, pb + m, qt * 128:(qt + 1) * 128], rhs=Vh[:, kt, :],
                                start=(kt == 0 and a % 3 == 0), stop=(kt == KT_ALL - 1), skip_group_check=True),
                                reads=['PT%d' % (pb + m), kvn], writes=['B%d' % bki])
                for b_ in range(3):
                    na = 3 if b_ < 2 else 2
                    P.add(DVE, R.tensor_copy(out=accs[:, b_ * 3:b_ * 3 + na, :], in_=banks[4 + b_][:, 0:na * 129].rearrange("p (a e) -> p a e", e=129)),
                          reads=['B%d' % (4 + b_)], writes=accn)
                st = stat[:, 80:96]
                o3 = tmp[:, 3, :].rearrange("p (q e) -> p q e", q=4)
                sq3 = tmp[:, 4, :].rearrange("p (q e) -> p q e", q=4)
                P.add(DVE, R.reciprocal(out=st[:, 0:8], in_=accs[:, :, 128]), reads=accn, writes=['statD'])
                P.add(DVE, R.tensor_scalar(out=st[:, 1:8:2], in0=st[:, 1:8:2], scalar1=neglam, scalar2=None, op0=ALU.mult), reads=['statD', 'lams'], writes=['statD'])
                P.add(DVE, R.tensor_tensor(out=accs[:, :, 0:128], in0=accs[:, :, 0:128], in1=bc(st[:, 0:8].unsqueeze(2), [128, 8, 128]), op=ALU.mult),
                      reads=accn + ['statD'], writes=accn)
                P.add(POOL, R.tensor_tensor(out=o3, in0=accs[:, 0:8:2, 0:128], in1=accs[:, 1:8:2, 0:128], op=ALU.add), reads=accn, writes=['tmp3'])
                P.add(POOL, R.tensor_tensor(out=sq3, in0=o3, in1=o3, op=ALU.mult), reads=['tmp3'], writes=['tmp4'])
                P.add(DVE, R.tensor_reduce(out=st[:, 8:12], in_=sq3, axis=AX.X, op=ALU.add), reads=['tmp4'], writes=['statD'])
                rsqrt_col(st[:, 8:12], st[:, 8:12], 1.0 / 128, EPS, 4, 'statD')
                P.add(DVE, R.tensor_tensor(out=o3, in0=o3, in1=bc(st[:, 8:12].unsqueeze(2), [128, 4, 128]), op=ALU.mult), reads=['tmp3', 'statD'], writes=['tmp3'])
                P.add(POOL, R.tensor_tensor(out=ybf[:, :, 512 + h * 128:640 + h * 128], in0=o3,
                                            in1=bc(subT[:, h * 128:(h + 1) * 128].unsqueeze(1), [128, 4, 128]), op=ALU.mult),
                      reads=['tmp3', 'subT'], writes=['ybf'])
            ckpt('a%dx' % n)
            for c in range(8):
                for t in range(4):
                    P.add(PE, R.transpose(out=BT[:, t * 128:(t + 1) * 128], in_=ybf[:, t, c * 128:(c + 1) * 128], identity=identb[:]),
                          reads=['ybf', 'identb'], writes=['BT'])
                P.add(ACT, R.copy(out=hT[:, c, :], in_=BT[:, 0:512]), reads=['BT'], writes=['hT'])
            wov = woutb_d.rearrange("(k p) c -> p k c", p=128)
            wo = {}

            def wsrc(dt):
                g = dt // 4
                if g not in wo:
                    wo[g] = ws.get(P, wov[:, :, g * 512:(g + 1) * 512], _view_k8, res='wout_b')
                return wo[g][0], wo[g][1], (dt % 4) * 128
            fm_proj_residual(slot, wsrc, 8, lambda k: hT[:, k, :], ['hT'], cols[:, 6, :], _view_k8, ws)
            if n + 1 < NB:
                prefetch(n + 1)
            ckpt('a%do' % n)
            hs = n % 2
            norm_to_T(slot, 4, lambda k: h2T[:, hs, k, :], 1, 'h2T%d' % hs, cols[:, 4, :], cols[:, 5, :])
            if n > 0:
                P.add(POOL, R.tensor_copy(out=h2T[:, 1 - hs, :, 513:514], in_=h2T[:, hs, :, 1:2]), reads=['h2T%d' % hs], writes=['h2T%d' % (1 - hs)])
                P.add(POOL, R.tensor_copy(out=h2T[:, hs, :, 0:1], in_=h2T[:, 1 - hs, :, 512:513]), reads=['h2T%d' % (1 - hs)], writes=['h2T%d' % hs])
            else:
                P.add(POOL, R.memset(h2T[:, hs, :, 0:1], 0.0), writes=['h2T%d' % hs])
            if n == NB - 1:
                P.add(POOL, R.memset(h2T[:, hs, :, 513:514], 0.0), writes=['h2T%d' % hs])

        mod_finish()
        ckpt('p1')
        prefetch(0)
        for n in range(NB):
            attn_block(n)
            ckpt('a%d' % n)
            hook = None
            if n > 0:
                ffn(n - 1, ws, hook)
                ckpt('f%d' % (n - 1))
        ffn(NB - 1, ws)

    Pd = Prog(nc)
    wsd = WStream(ring, 2)
    try:
        gen(Pd, wsd)
    except _Stop:
        pass
    P = Prog(nc)
    ws = WStream(ring, 2, plan=wsd.rec)
    try:
        gen(P, ws)
    except _Stop:
        pass
    P.emit()
    return nc, P


_CACHE = {}


def _consts():
    f32 = np.float32
    ident = np.eye(128, dtype=f32)
    j = np.arange(128)[:, None]
    i = np.arange(128)[None, :]
    cmask = np.concatenate([(j <= i).astype(f32), (j >= i).astype(f32)], axis=1)
    p = np.arange(128, dtype=f32)[:, None]
    one4 = np.ones((1, 4), f32)
    ksc = np.concatenate([(p + 1) * one4, (128 - p) * one4], axis=1)
    kdec = np.concatenate([-(127 - p) * one4, -p * one4], axis=1)
    cdec = np.full((128, 8), -128.0, f32)
    decexp = np.concatenate([ksc, -ksc, kdec, cdec], axis=1).astype(f32)

    def rope(hd):
        nf = hd // 4
        inv = (10000.0 ** (-(np.arange(nf, dtype=f32) / f32(nf)))).astype(f32)
        pos = np.arange(L)
        row = (pos // 64).astype(f32)
        col = (pos % 64).astype(f32)
        ang = np.concatenate([row[:, None] * inv, col[:, None] * inv], axis=-1).astype(f32)
        return np.concatenate([np.cos(ang), np.sin(ang)], axis=-1).astype(f32)
    return dict(ident=ident, cmask=cmask, decexp=decexp, rope_r=rope(128), rope_d=rope(64))


def _colmajor(v, k):
    return np.ascontiguousarray(np.asarray(v, np.float32).reshape(k, 128).T)


def kernel(x, c, ctx, c_ctx, w_mod, b_mod, norm1_g, w_in, ret_decay_logit, ret_gn_g,
           diff_lambda, diff_subln_g, w_out, norm2_g, w_up, conv_w, conv_b, w_down, final_g):
    if 'nc' not in _CACHE:
        _CACHE['nc'] = build_program()
    nc, _ = _CACHE['nc']
    in_maps = _prep(x, c, ctx, c_ctx, w_mod, b_mod, norm1_g, w_in, ret_decay_logit, ret_gn_g,
                    diff_lambda, diff_subln_g, w_out, norm2_g, w_up, conv_w, conv_b, w_down, final_g)
    res = run_bass_kernel_spmd(nc, in_maps, core_ids=list(range(8)))
    return np.stack([np.asarray(r['out'], np.float32) for r in res.results], axis=0)


def _prep(x, c, ctx, c_ctx, w_mod, b_mod, norm1_g, w_in, ret_decay_logit, ret_gn_g,
          diff_lambda, diff_subln_g, w_out, norm2_g, w_up, conv_w, conv_b, w_down, final_g):
    cst = _consts()
    f32 = np.float32
    x = np.asarray(x, f32)
    ctx = np.asarray(ctx, f32)
    c = np.asarray(c, f32)
    shared = dict(
        w_mod=np.ascontiguousarray(np.asarray(w_mod, f32)[0]),
        bmodc=_colmajor(np.asarray(b_mod)[0], 48),
        n1g=_colmajor(np.asarray(norm1_g)[0], 8),
        n2g=_colmajor(np.asarray(norm2_g)[0], 8),
        w_in=np.ascontiguousarray(np.asarray(w_in, f32)[0]),
        dlog=np.ascontiguousarray(np.asarray(ret_decay_logit, f32)[0].reshape(1, 8)),
        gng=np.ascontiguousarray(np.asarray(ret_gn_g, f32)[0].reshape(1, 1024)),
        dlam=np.ascontiguousarray(np.asarray(diff_lambda, f32)[0].reshape(1, 256)),
        subg=np.ascontiguousarray(np.asarray(diff_subln_g, f32)[0].reshape(1, 512)),
        w_out=np.ascontiguousarray(np.asarray(w_out, f32)[0]),
        w_up=np.ascontiguousarray(np.asarray(w_up, f32)[0]),
        convwc=np.ascontiguousarray(np.asarray(conv_w, f32)[0].reshape(3, 44, 128).transpose(2, 0, 1).reshape(128, 132)),
        convbc=_colmajor(np.asarray(conv_b)[0], 44),
        w_down=np.ascontiguousarray(np.asarray(w_down, f32)[0]),
        fing=np.ascontiguousarray(np.asarray(final_g, f32).reshape(1, D)),
        **cst,
    )
    cc = _colmajor(np.asarray(c_ctx), 8)
    in_maps = []
    for b in range(8):
        m = dict(shared)
        m['x'] = np.ascontiguousarray(x[b])
        m['ctx'] = np.ascontiguousarray(ctx[b])
        m['ccol'] = np.ascontiguousarray(np.concatenate([_colmajor(c[b], 8), cc], axis=1))
        in_maps.append(m)
    return in_maps
```

```python
import numpy as np
import concourse.bass as bass
import concourse.mybir as mybir
from concourse.bass_utils import run_bass_kernel_spmd

F32 = mybir.dt.float32
BF16 = mybir.dt.bfloat16
AF = mybir.ActivationFunctionType
ALU = mybir.AluOpType
AX = mybir.AxisListType

D = 1024
L = 4096
LC = 256
NB = 8
DFF = 2816
NF = 22
EPS = 1e-6
GN_EPS = 1e-5
LAMBDA_INIT = 0.8 - 0.6 * 1.0
KT_ALL = 34
DEBUG = False
SERIAL = False
PROFILE_SCOPES = False
STOP = None


class _Stop(Exception):
    pass


class _Rec:
    def __getattr__(self, name):
        def f(*a, **kw):
            return (name, a, kw)
        return f


R = _Rec()


class Prog:
    def __init__(self, nc):
        self.nc = nc
        self.ops = []
        self.alias = {}
        self.phase = None

    def add(self, eng, fn, reads=(), writes=(), dma=None):
        if SERIAL is True or (SERIAL and (eng in SERIAL or (eng == 'pool' and ('pooldma' if dma else 'poolcmp') in SERIAL))):
            writes = tuple(writes) + ('__GLOBAL__',)
        self.ops.append(dict(eng=eng, fn=fn, reads=tuple(reads), writes=tuple(writes), dma=dma, phase=self.phase))

    def _exp(self, names):
        out = []
        for n in names:
            base = n.split(':')[0]
            if base in self.alias:
                out.extend(self.alias[base])
            else:
                out.append(n)
        return out

    def emit(self):
        nc = self.nc
        engs = {'pe': nc.tensor, 'act': nc.scalar, 'dve': nc.vector, 'pool': nc.gpsimd, 'sp': nc.sync}
        ops = self.ops
        last_w = {}
        readers = {}
        for i, o in enumerate(ops):
            deps = set()
            rds = self._exp(o['reads'])
            wrs = self._exp(o['writes'])
            for r in rds:
                if r in last_w:
                    deps.add(last_w[r])
                if r[0] == 'B':
                    for (re_, rd_), j in readers.get(r, {}).items():
                        if re_ != o['eng']:
                            deps.add(j)
            for w in wrs:
                if w in last_w:
                    deps.add(last_w[w])
                for j in readers.get(w, {}).values():
                    deps.add(j)
            deps.discard(i)
            o['deps'] = deps
            for r in rds:
                readers.setdefault(r, {})[(o['eng'], o['dma'])] = i
            for w in wrs:
                last_w[w] = i
                readers[w] = {}
        signal = [False] * len(ops)
        for i, o in enumerate(ops):
            nd = set()
            for j in o['deps']:
                p = ops[j]
                if p['dma'] is None and o['dma'] is None and p['eng'] == 'pe' and o['eng'] == 'pe':
                    continue
                nd.add(j)
                signal[j] = True
            o['deps'] = nd
        esem = {k: nc.alloc_semaphore('s_' + k) for k in engs}
        ecnt = {k: 0 for k in engs}
        dsem = {}
        dcnt = {}
        waited = {k: {} for k in engs}
        ev = [None] * len(ops)
        final = {}
        cur_scope = None
        for i, o in enumerate(ops):
            if PROFILE_SCOPES and o['phase'] != (cur_scope[0] if cur_scope else None):
                if cur_scope:
                    nc.leave_named_scope(cur_scope[0], cur_scope[1], False)
                    cur_scope = None
                if o['phase']:
                    sid, _ = nc.enter_named_scope(o['phase'], False)
                    cur_scope = (o['phase'], sid)
            E = o['eng']
            e = engs[E]
            need = {}
            for j in o['deps']:
                s, v, nm = ev[j]
                if need.get(nm, (None, 0))[1] < v:
                    need[nm] = (s, v)
            for nm, (s, v) in need.items():
                if waited[E].get(nm, 0) >= v:
                    continue
                e.wait_ge(s, v)
                waited[E][nm] = v
            name, a_, kw_ = o['fn']
            inst = getattr(e, name)(*a_, **kw_)
            if o['dma'] is not None:
                k = o['dma']
                if k not in dsem:
                    dsem[k] = nc.alloc_semaphore('d_' + k)
                    dcnt[k] = 0
                dcnt[k] += 16
                inst.then_inc(dsem[k], 16)
                ev[i] = (dsem[k], dcnt[k], 'd_' + k)
                final['d_' + k] = (dsem[k], dcnt[k])
            elif signal[i]:
                ecnt[E] += 1
                inst.then_inc(esem[E], 1)
                ev[i] = (esem[E], ecnt[E], 's_' + E)
        if cur_scope:
            nc.leave_named_scope(cur_scope[0], cur_scope[1], False)
        for nm, (s, v) in final.items():
            if waited['sp'].get(nm, 0) < v:
                nc.sync.wait_ge(s, v)
        self.n_ops = len(ops)
        self.ecnt = ecnt
        self.dcnt = dcnt


class WStream:
    def __init__(self, ring, nslots, plan=None):
        self.ring = ring
        self.nslots = nslots
        self.plan = plan
        self.rec = []
        self.i = 0
        self.issued = 0

    def _issue(self, P, upto):
        while self.issued <= upto and self.issued < len(self.plan):
            j = self.issued
            parts, view, res = self.plan[j]
            s = j % self.nslots
            for src, sub in parts:
                dst = sub(view(self.ring, s))
                P.add('sp', R.dma_start(out=dst, in_=src), reads=[res], writes=['wr%d' % s], dma='wr%d' % s)
            self.issued += 1

    def get(self, P, src, view, res='win_b'):
        parts = src if isinstance(src, list) else [(src, lambda v: v)]
        j = self.i
        self.i += 1
        s = j % self.nslots
        if self.plan is None:
            self.rec.append((parts, view, res))
        else:
            self._issue(P, j + self.nslots - 1)
        return view(self.ring, s), 'wr%d' % s


def _view_k8(ring, s):
    return ring[:, s, 0:4096].rearrange("p (k c) -> p k c", k=8)


def _view_k8n(n):
    def f(ring, s):
        return ring[:, s, 0:8 * n].rearrange("p (k c) -> p k c", k=8)
    return f


def _view_dn(ring, s):
    return ring[:, s, 0:NF * 128].rearrange("p (k c) -> p k c", k=NF)


def build_program():
    nc = bass.Bass("TRN2", target_bir_lowering=False)

    def din(name, shape, dt=F32):
        return nc.dram_tensor(name, list(shape), dt, kind="ExternalInput").ap()

    x_d = din("x", [L, D])
    ctx_d = din("ctx", [LC, D])
    ccol_d = din("ccol", [128, 16])
    wmod_d = din("w_mod", [D, 6 * D])
    bmod_d = din("bmodc", [128, 48])
    n1g_d = din("n1g", [128, 8])
    n2g_d = din("n2g", [128, 8])
    win_d = din("w_in", [D, 3584])
    dlog_d = din("dlog", [1, 8])
    gng_d = din("gng", [1, 1024])
    dlam_d = din("dlam", [1, 256])
    subg_d = din("subg", [1, 512])
    wout_d = din("w_out", [D, D])
    wup_d = din("w_up", [D, 2 * DFF])
    convw_d = din("convwc", [128, 3 * 44])
    convb_d = din("convbc", [128, 44])
    wdn_d = din("w_down", [DFF, D])
    fing_d = din("fing", [1, D])
    ident_d = din("ident", [128, 128])
    cm_d = din("cmask", [128, 256])
    decexp_d = din("decexp", [128, 32])
    ropeR_d = din("rope_r", [L, 128])
    ropeD_d = din("rope_d", [L, 64])
    out_d = nc.dram_tensor("out", [L, D], F32, kind="ExternalOutput").ap()
    KTd = nc.dram_tensor("scr_kt", [4, 128, KT_ALL * 128], BF16).ap()
    Vd = nc.dram_tensor("scr_v", [4, 128, KT_ALL * 129], BF16).ap()
    Sbd = nc.dram_tensor("scr_sb", [32, 128, 512], BF16).ap()
    KRd = nc.dram_tensor("scr_kr", [8, 128, 2048], BF16).ap()
    RVd = nc.dram_tensor("scr_rv", [8, 128, 2048], BF16).ap()
    HTd = nc.dram_tensor("scr_ht", [8, 128, 4096], BF16).ap()
    winb_d = nc.dram_tensor("scr_win", [D, 3584], BF16).ap()
    woutb_d = nc.dram_tensor("scr_wout", [D, D], BF16).ap()
    wupb_d = nc.dram_tensor("scr_wup", [D, 2 * DFF], BF16).ap()
    wdnb_d = nc.dram_tensor("scr_wdn", [DFF, D], BF16).ap()
    dbg = {}

    def sb(name, shape, dt=F32):
        return nc.alloc_sbuf_tensor("s_" + name, list(shape), dt)

    identb = sb("identb", [128, 128], BF16)
    identf = sb("identf", [128, 128])
    cm = sb("cm", [128, 256])
    maskF = sb("maskF", [128, 4, 128])
    maskB = sb("maskB", [128, 4, 128])
    decexp = sb("decexp", [128, 32])
    dect = sb("dect", [128, 32])
    dbase = sb("dbase", [128, 8])
    gnT = sb("gnT", [128, 1024])
    subT = sb("subT", [128, 512])
    finT = sb("finT", [128, 1024])
    modc = sb("modc", [128, 48, 2])
    cols = sb("cols", [128, 8, 8])
    ccol = sb("ccol", [128, 16])
    scol = sb("scol", [128, 16])
    bmodc = sb("bmodc", [128, 48])
    n1g = sb("n1g", [128, 8])
    n2g = sb("n2g", [128, 8])
    convw = sb("convw", [128, 3, 44])
    convb = sb("convb", [128, 44])
    lamt = sb("lamt", [128, 256])
    lams = sb("lams", [128, 8])
    mhalf = sb("mhalf", [128, 16])
    stat = sb("stat", [128, 96])
    Sf = sb("Sf", [128, 512])
    Sfb = sb("Sfb", [128, 2, 512], BF16)
    Sbs = sb("Sbs", [128, 512])
    Sbb = sb("Sbb", [128, 4, 512], BF16)
    xblk = sb("xblk", [128, 2, 4, 1024])
    xst = sb("xst", [128, 1024])
    ybf = sb("ybf", [128, 4, 1024], BF16)
    hT = sb("hT", [128, 8, 512], BF16)
    h2T = sb("h2T", [128, 2, 8, 514], BF16)
    qr = sb("qr", [128, 4, 512], BF16)
    kr = sb("kr", [128, 4, 512], BF16)
    rv = sb("rv", [128, 4, 512], BF16)
    vpf = sb("vpf", [128, 4, 512], BF16)
    sg = sb("sg", [128, 4, 512])
    qT = sb("qT", [128, 2, 512], BF16)
    kTt = sb("kTt", [128, 2, 512], BF16)
    AfT = sb("AfT", [128, 2, 512], BF16)
    AbT = sb("AbT", [128, 2, 512], BF16)
    sbT = sb("sbT", [128, 4, 512], BF16)
    tmp = sb("tmp", [128, 6, 512])
    ropeR = sb("ropeR", [128, 4, 128])
    ropeD = sb("ropeD", [128, 4, 64])
    PT = sb("PT", [128, 4, 512], BF16)
    Vst = sb("Vst", [128, 4, 4, 129], BF16)
    ring = sb("ring", [128, 2, 4096], BF16)
    KVA = sb("KVA", [128, 17920], BF16)
    actT = KVA[:, 0:NF * 512].rearrange("p (f t) -> p f t", f=NF)

    def kvK(s):
        return KVA[:, s * 8960: s * 8960 + 4352]

    def kvV(s):
        return KVA[:, s * 8960 + 4352: s * 8960 + 4352 + KT_ALL * 129].rearrange("p (k e) -> p k e", k=KT_ALL)

    SC = nc.alloc_psum_tensor("SC", [128, 2048], F32)
    banks = [SC[:, i * 512:(i + 1) * 512] for i in range(4)] + [nc.alloc_psum_tensor("B%d" % i, [128, 512], F32)[:, :] for i in range(4, 7)]
    BT = nc.alloc_psum_tensor("BT", [128, 1024], BF16)

    wplan = {'plan': None}

    def gen(P, ws):
        P.alias['actT'] = ['kv0', 'kv1']
        SP, PE, ACT, DVE, POOL = 'sp', 'pe', 'act', 'dve', 'pool'

        def bc(ap, shape):
            return ap.broadcast_to(list(shape))

        ld_first = True
        for src_, dst_, nrow, key in ((win_d, winb_d, D, 'win_b'), (wout_d, woutb_d, D, 'wout_b'), (wup_d, wupb_d, D, 'wup_b'), (wdn_d, wdnb_d, DFF, 'wdn_b')):
            for r0 in range(0, nrow, 256):
                P.add(POOL, R.dma_start(out=dst_[r0:r0 + 256, :].rearrange("(p a) c -> p a c", p=128),
                                        in_=src_[r0:r0 + 256, :].rearrange("(p a) c -> p a c", p=128)), writes=[key], dma='wc_' + key)
        def ld(dst, src, name, eng=SP):
            P.add(eng, R.dma_start(out=dst, in_=src), writes=[name], dma=name)
        ld(identf[:], ident_d, 'identf')
        ld(identb[:], ident_d, 'identb', POOL)
        ld(cm[:], cm_d, 'cm')
        ld(decexp[:], decexp_d, 'decexp')
        ld(ccol[:], ccol_d, 'ccol')
        ld(bmodc[:], bmod_d, 'bmodc')
        ld(n1g[:], n1g_d, 'n1g')
        ld(n2g[:], n2g_d, 'n2g')
        ld(convw[:].rearrange("p a b -> p (a b)"), convw_d, 'convw')
        ld(convb[:], convb_d, 'convb')
        ld(dbase[:], bc(dlog_d, [128, 8]), 'dbase')
        ld(gnT[:], bc(gng_d, [128, 1024]), 'gnT')
        ld(subT[:], bc(subg_d, [128, 512]), 'subT')
        ld(finT[:], bc(fing_d, [128, 1024]), 'finT')
        ld(lamt[:], bc(dlam_d, [128, 256]), 'lamt')
        P.add(DVE, R.memset(mhalf[:], -0.5), writes=['mhalf'])
        P.add(DVE, R.memset(Vst[:], 1.0), writes=['Vst'])
        P.add(DVE, R.memset(Sf[:], 0.0), writes=['Sf'])
        P.add(DVE, R.memset(Sbs[:], 0.0), writes=['Sbs'])
        P.add(DVE, R.memset(Sfb[:], 0.0), writes=['Sfb0', 'Sfb1'])
        P.add(DVE, R.memset(Sbb[:], 0.0), writes=['Sbb0', 'Sbb1', 'Sbb2', 'Sbb3'])
        P.add(DVE, R.memset(h2T[:], 0.0), writes=['h2T0', 'h2T1'])
        P.add(POOL, R.tensor_scalar(out=gnT[:], in0=gnT[:], scalar1=0.5, scalar2=None, op0=ALU.mult),
              reads=['gnT'], writes=['gnT'])
        P.add(POOL, R.tensor_scalar(out=subT[:], in0=subT[:], scalar1=1.0 - LAMBDA_INIT, scalar2=None, op0=ALU.mult),
              reads=['subT'], writes=['subT'])
        P.add(ACT, R.activation(out=dbase[:], in_=dbase[:], func=AF.Exp, scale=-1.0), reads=['dbase'], writes=['dbase'])
        P.add(POOL, R.tensor_scalar(out=dbase[:], in0=dbase[:], scalar1=1.0, scalar2=None, op0=ALU.add),
              reads=['dbase'], writes=['dbase'])
        P.add(POOL, R.tensor_tensor(out=dect[:].rearrange("p (a b) -> p a b", a=4),
                                              in0=bc(dbase[:].unsqueeze(1), [128, 4, 8]),
                                              in1=decexp[:].rearrange("p (a b) -> p a b", a=4), op=ALU.pow),
              reads=['dbase', 'decexp'], writes=['dect'])
        P.add(POOL, R.tensor_scalar(out=dect[:, 0:8], in0=dect[:, 0:8], scalar1=128.0 ** -0.5, scalar2=None, op0=ALU.mult),
              reads=['dect'], writes=['dect'])
        P.add(POOL, R.tensor_scalar(out=dect[:, 16:24], in0=dect[:, 16:24], scalar1=128.0 ** -0.5, scalar2=None, op0=ALU.mult),
              reads=['dect'], writes=['dect'])
        for h in range(4):
            P.add(DVE, R.tensor_scalar(out=maskF[:, h, :], in0=cm[:, 0:128], scalar1=dect[:, h:h + 1], scalar2=None, op0=ALU.mult),
                  reads=['cm', 'dect'], writes=['maskF'])
            P.add(DVE, R.tensor_scalar(out=maskB[:, h, :], in0=cm[:, 128:256], scalar1=dect[:, 4 + h:5 + h], scalar2=None, op0=ALU.mult),
                  reads=['cm', 'dect'], writes=['maskB'])
        P.add(DVE, R.tensor_tensor(out=lamt[:, 0:64], in0=lamt[:, 0:64], in1=lamt[:, 64:128], op=ALU.mult), reads=['lamt'], writes=['lamt'])
        P.add(DVE, R.tensor_tensor(out=lamt[:, 128:192], in0=lamt[:, 128:192], in1=lamt[:, 192:256], op=ALU.mult), reads=['lamt'], writes=['lamt'])
        P.add(DVE, R.tensor_reduce(out=lams[:, 0:1], in_=lamt[:, 0:64], axis=AX.X, op=ALU.add), reads=['lamt'], writes=['lams'])
        P.add(DVE, R.tensor_reduce(out=lams[:, 1:2], in_=lamt[:, 128:192], axis=AX.X, op=ALU.add), reads=['lamt'], writes=['lams'])
        P.add(ACT, R.activation(out=lams[:, 2:4], in_=lams[:, 0:2], func=AF.Exp), reads=['lams'], writes=['lams'])
        P.add(DVE, R.tensor_tensor(out=lams[:, 4:5], in0=lams[:, 3:4], in1=lams[:, 2:3], op=ALU.subtract), reads=['lams'], writes=['lams'])
        P.add(DVE, R.tensor_scalar(out=lams[:, 5:6], in0=lams[:, 4:5], scalar1=-LAMBDA_INIT, scalar2=None, op0=ALU.add), reads=['lams'], writes=['lams'])
        neglam = lams[:, 5:6]

        P.add(ACT, R.activation(out=scol[:], in_=ccol[:], func=AF.Tanh, scale=0.5), reads=['ccol'], writes=['scol'])
        P.add(DVE, R.scalar_tensor_tensor(out=scol[:], in0=scol[:], scalar=1.0, in1=ccol[:], op0=ALU.add, op1=ALU.mult),
              reads=['scol', 'ccol'], writes=['scol'])
        P.add(DVE, R.tensor_scalar(out=scol[:], in0=scol[:], scalar1=0.5, scalar2=None, op0=ALU.mult), reads=['scol'], writes=['scol'])
        scv = scol[:].rearrange("p (a k) -> p k a", a=2)
        wm_v = wmod_d.rearrange("(k p) c -> p k c", p=128)
        def mod_load(g, stage, rn, key):
            P.add(SP, R.dma_start(out=stage, in_=wm_v[:, :, g * 512:(g + 1) * 512]), writes=[rn], dma=key)

        def mod_mm(g, stage, rn, bank, bname, col0):
            for ct in range(4):
                j = g * 4 + ct
                for k in range(8):
                    P.add(PE, R.matmul(out=bank[:, 2 * j - col0:2 * j - col0 + 2], lhsT=stage[:, k, ct * 128:(ct + 1) * 128], rhs=scv[:, k, :],
                                       start=(k == 0), stop=(k == 7)), reads=[rn, 'scol'], writes=[bname])

        for g in range(4):
            s = g % 2
            stage = xblk[:, s, :, :].rearrange("p a b -> p (a b)").rearrange("p (k c) -> p k c", k=8)
            mod_load(g, stage, 'xblk%d' % s, 'xblk%d' % s)
            mod_mm(g, stage, 'xblk%d' % s, banks[0], 'B0', 0)
        P.add(DVE, R.tensor_tensor(out=modc[:, 0:16, :], in0=banks[0][:, 0:32].rearrange("p (j a) -> p j a", a=2),
                                   in1=bc(bmodc[:, 0:16].unsqueeze(2), [128, 16, 2]), op=ALU.add),
              reads=['B0', 'bmodc'], writes=['modc'])
        def mcol(which, a):
            return modc[:, which * 8:(which + 1) * 8, a]
        for a, (ia, ish) in enumerate([(0, 2), (1, 3)]):
            P.add(DVE, R.scalar_tensor_tensor(out=cols[:, ia, :], in0=mcol(1, a), scalar=1.0, in1=n1g[:], op0=ALU.add, op1=ALU.mult),
                  reads=['modc', 'n1g'], writes=['cols'])
            P.add(DVE, R.tensor_copy(out=cols[:, ish, :], in_=mcol(0, a)), reads=['modc'], writes=['cols'])

        def mod_stage2(g):
            s_ = g % 2
            return KVA[:, s_ * 8192:(s_ + 1) * 8192].bitcast(F32).rearrange("p (k c) -> p k c", k=8), 'kv%d' % s_, 'modst%d' % s_

        def mod_part2_step(i):
            if i == 0:
                st_, rn_, key_ = mod_stage2(4)
                mod_load(4, st_, rn_, key_)
            if 4 + i + 1 < 12:
                st_, rn_, key_ = mod_stage2(4 + i + 1)
                mod_load(4 + i + 1, st_, rn_, key_)
            if 4 + i < 12:
                st_, rn_, key_ = mod_stage2(4 + i)
                mod_mm(4 + i, st_, rn_, banks[6], 'B6', 32)

        def mod_finish():
            P.add(DVE, R.tensor_tensor(out=modc[:, 16:48, :], in0=banks[6][:, 0:64].rearrange("p (j a) -> p j a", a=2),
                                       in1=bc(bmodc[:, 16:48].unsqueeze(2), [128, 32, 2]), op=ALU.add),
                  reads=['B6', 'bmodc'], writes=['modc2'])
            P.add(DVE, R.scalar_tensor_tensor(out=cols[:, 4, :], in0=mcol(4, 0), scalar=1.0, in1=n2g[:], op0=ALU.add, op1=ALU.mult),
                  reads=['modc2', 'n2g'], writes=['cols2'])
            P.add(DVE, R.tensor_copy(out=cols[:, 5, :], in_=mcol(3, 0)), reads=['modc2'], writes=['cols2'])
            P.add(DVE, R.tensor_copy(out=cols[:, 6, :], in_=mcol(2, 0)), reads=['modc2'], writes=['cols2'])
            P.add(DVE, R.tensor_scalar(out=cols[:, 7, :], in0=mcol(5, 0), scalar1=0.5, scalar2=None, op0=ALU.mult), reads=['modc2'], writes=['cols2'])

        def ckpt(name):
            P.phase = name
            if STOP == name:
                raise _Stop()
        ckpt('mod')
        cnt = {'bt': 0, 'st': 0}

        def rsqrt_col(dst, src, scale, eps, n, rn='stat'):
            P.add(POOL, R.tensor_scalar(out=dst, in0=src, scalar1=scale, scalar2=eps, op0=ALU.mult, op1=ALU.add),
                  reads=[rn], writes=[rn])
            P.add(POOL, R.tensor_tensor(out=dst, in0=dst, in1=mhalf[:, 0:n], op=ALU.pow), reads=[rn, 'mhalf'], writes=[rn])

        def load_x(slot, src_ap, nt, extra=None):
            P.add(SP, R.dma_start(out=xblk[:, slot, 0:nt, :], in_=src_ap.rearrange("(t p) d -> p t d", p=128)),
                  writes=['xblk%d' % slot], dma='xblk%d' % slot)

        def norm_stats(slot, nt):
            xn = 'xblk%d' % slot
            for t in range(nt):
                P.add(ACT, R.activation(out=ybf[:, t, :], in_=xblk[:, slot, t, :], func=AF.Square,
                                        accum_out=stat[:, 8 + t:9 + t]), reads=[xn], writes=['ybf', 'stat'])
            rsqrt_col(stat[:, 8:8 + nt], stat[:, 8:8 + nt], 1.0 / D, EPS, nt)
            for t in range(nt):
                P.add(DVE, R.tensor_scalar(out=ybf[:, t, :], in0=xblk[:, slot, t, :], scalar1=stat[:, 8 + t:9 + t],
                                                                   scalar2=None, op0=ALU.mult), reads=[xn, 'stat'], writes=['ybf'])

        def norm_stream_load(src_ap, t):
            P.add(SP, R.dma_start(out=xst[:], in_=src_ap[t * 128:(t + 1) * 128, :]), writes=['xst'], dma='xst')

        def norm_stream_tile(t):
            P.add(ACT, R.activation(out=ybf[:, t, :], in_=xst[:], func=AF.Square, accum_out=stat[:, 12 + t:13 + t]),
                  reads=['xst'], writes=['ybf', 'statS%d' % t])
            rsqrt_col(stat[:, 12 + t:13 + t], stat[:, 12 + t:13 + t], 1.0 / D, EPS, 1, 'statS%d' % t)
            P.add(DVE, R.tensor_scalar(out=ybf[:, t, :], in0=xst[:], scalar1=stat[:, 12 + t:13 + t], scalar2=None, op0=ALU.mult),
                  reads=['xst', 'statS%d' % t], writes=['ybf'])

        def norm_stats_stream(src_ap, nt):
            for t in range(nt):
                norm_stream_load(src_ap, t)
                norm_stream_tile(t)

        def norm_tr(nt, dstT, dst_off, dst_name, acol, shcol):
            for k in range(8):
                for t in range(nt):
                    P.add(PE, R.transpose(out=BT[:, t * 128:(t + 1) * 128], in_=ybf[:, t, k * 128:(k + 1) * 128], identity=identb[:]),
                          reads=['ybf', 'identb'], writes=['BT'])
                P.add(ACT, R.activation(out=dstT(k)[:, dst_off:dst_off + nt * 128], in_=BT[:, 0:nt * 128], func=AF.Identity,
                                        bias=shcol[:, k:k + 1], scale=acol[:, k:k + 1]),
                      reads=['BT', 'cols', 'cols2'], writes=[dst_name])

        def norm_to_T(slot, nt, dstT, dst_off, dst_name, acol, shcol):
            norm_stats(slot, nt)
            norm_tr(nt, dstT, dst_off, dst_name, acol, shcol)

        def proj_tm(nt, g, evac, ws_):
            wv, wn = ws_.get(P, winb_d.rearrange("(k p) c -> p k c", p=128)[:, :, g * 512:(g + 1) * 512], _view_k8)
            for t in range(nt):
                bi = cnt['bt'] % 2
                cnt['bt'] += 1
                bk, bn = banks[bi], 'B%d' % bi
                for k in range(8):
                    P.add(PE, R.matmul(out=bk[:, :], lhsT=hT[:, k, t * 128:(t + 1) * 128], rhs=wv[:, k, :],
                                                                  start=(k == 0), stop=(k == 7)), reads=['hT', wn], writes=[bn])
                evac(t, bk, bn)

        def rope_evac(dst, table, nh, half, name, tname):
            def f(t, bk, bn):
                v = bk[:, :].rearrange("p (h a d) -> p h a d", h=nh, a=2)
                x1, x2 = v[:, :, 0, :], v[:, :, 1, :]
                cs = bc(table[:, t, 0:half].unsqueeze(1), [128, nh, half])
                sn = bc(table[:, t, half:2 * half].unsqueeze(1), [128, nh, half])
                o = dst[:, t, :].rearrange("p (h a d) -> p h a d", h=nh, a=2)
                tv = [tmp[:, i, 0:256].rearrange("p (h d) -> p h d", h=nh) for i in range(4)]
                P.add(DVE, R.tensor_tensor(out=tv[0], in0=x1, in1=cs, op=ALU.mult), reads=[bn, tname], writes=['tmp0'])
                P.add(DVE, R.tensor_tensor(out=tv[1], in0=x2, in1=sn, op=ALU.mult), reads=[bn, tname], writes=['tmp1'])
                P.add(DVE, R.tensor_tensor(out=tv[2], in0=x1, in1=sn, op=ALU.mult), reads=[bn, tname], writes=['tmp2'])
                P.add(DVE, R.tensor_tensor(out=tv[3], in0=x2, in1=cs, op=ALU.mult), reads=[bn, tname], writes=['tmp3'])
                P.add(DVE, R.tensor_tensor(out=o[:, :, 0, :], in0=tv[0], in1=tv[1], op=ALU.subtract), reads=['tmp0', 'tmp1'], writes=[name])
                P.add(DVE, R.tensor_tensor(out=o[:, :, 1, :], in0=tv[2], in1=tv[3], op=ALU.add), reads=['tmp2', 'tmp3'], writes=[name])
            return f

        def copy_evac(dst, name):
            def f(t, bk, bn):
                P.add(ACT, R.copy(out=dst[:, t, :], in_=bk[:, :]), reads=[bn], writes=[name])
            return f

        def vscale_evac(dsts):
            def f(t, bk, bn):
                for dst, name, co in dsts:
                    if co is None:
                        P.add(ACT, R.copy(out=dst[:, t, :], in_=bk[:, :]), reads=[bn], writes=[name])
                    else:
                        P.add(DVE, R.tensor_tensor(
                            out=dst[:, t, :].rearrange("p (h d) -> p h d", h=4), in0=bk[:, :].rearrange("p (h d) -> p h d", h=4),
                            in1=bc(dect[:, co:co + 4].unsqueeze(2), [128, 4, 128]), op=ALU.mult), reads=[bn, 'dect'], writes=[name])
            return f

        def state_update(ksrc, vsrc, t, S, cdec_off, sname, deps):
            for h in range(4):
                P.add(PE, R.matmul(out=banks[5][:, h * 128:(h + 1) * 128], lhsT=ksrc[:, t, h * 128:(h + 1) * 128],
                                                  rhs=vsrc[:, t, h * 128:(h + 1) * 128], start=True, stop=True), reads=deps, writes=['B5'])
            for h in range(4):
                P.add(DVE, R.scalar_tensor_tensor(out=S[:, h * 128:(h + 1) * 128], in0=S[:, h * 128:(h + 1) * 128],
                                                                 scalar=dect[:, cdec_off + h:cdec_off + h + 1], in1=banks[5][:, h * 128:(h + 1) * 128],
                                                                 op0=ALU.mult, op1=ALU.add), reads=['B5', sname, 'dect'], writes=[sname])

        def pass1_block(src_ap, nt, is_ctx, tile0, kt0, slot, nxt=None, next_tile0=None):
            nm_a, nm_s = (cols[:, 1, :], cols[:, 3, :]) if is_ctx else (cols[:, 0, :], cols[:, 2, :])
            norm_tr(nt, lambda k: hT[:, k, :], 0, 'hT', nm_a, nm_s)
            if not is_ctx:
                P.add(SP, R.dma_start(out=HTd[tile0 // 4], in_=hT[:].rearrange("p k t -> p (k t)")), reads=['hT'], writes=['HTd%d' % (tile0 // 4)], dma='st_ht')
            if nxt is not None:
                nxt()
            proj_tm(nt, 1, copy_evac(kr, 'kr') if is_ctx else rope_evac(kr, ropeR, 4, 64, 'kr', 'ropeR'), ws)
            if not is_ctx:
                P.add(SP, R.dma_start(out=KRd[tile0 // 4], in_=kr[:].rearrange("p t c -> p (t c)")), reads=['kr'], writes=['KRd%d' % (tile0 // 4)], dma='st_kr')
            if is_ctx:
                proj_tm(nt, 2, vscale_evac([(vpf, 'vpf', 16), (rv, 'rv', 20)]), ws)
            else:
                proj_tm(nt, 2, vscale_evac([(vpf, 'vpf', None), (rv, 'rv', 20)]), ws)
                P.add(SP, R.dma_start(out=RVd[tile0 // 4], in_=vpf[:].rearrange("p t c -> p (t c)")), reads=['vpf'], writes=['RVd%d' % (tile0 // 4)], dma='st_rv')
            proj_tm(nt, 5, copy_evac(sbT, 'sbT') if is_ctx else rope_evac(sbT, ropeD, 8, 32, 'sbT', 'ropeD'), ws)
            if next_tile0 is not None:
                t0 = next_tile0 * 128
                P.add(SP, R.dma_start(out=ropeR[:], in_=ropeR_d[t0:t0 + 512, :].rearrange("(t p) d -> p t d", p=128)), writes=['ropeR'], dma='ropeR')
                P.add(SP, R.dma_start(out=ropeD[:], in_=ropeD_d[t0:t0 + 512, :].rearrange("(t p) d -> p t d", p=128)), writes=['ropeD'], dma='ropeD')
            for h in range(4):
                for t in range(nt):
                    P.add(PE, R.transpose(out=BT[:, t * 128:(t + 1) * 128], in_=sbT[:, t, h * 128:(h + 1) * 128], identity=identb[:]),
                          reads=['sbT', 'identb'], writes=['BT'])
                P.add(ACT if h % 2 else DVE, (R.copy(out=qr[:, h, 0:nt * 128], in_=BT[:, 0:nt * 128])) if h % 2 else
                      (R.tensor_copy(out=qr[:, h, 0:nt * 128], in_=BT[:, 0:nt * 128])), reads=['BT'], writes=['qr'])
            P.add(SP, R.dma_start(out=KTd[:, :, kt0 * 128: kt0 * 128 + nt * 128].rearrange("h p t -> p h t"), in_=qr[:, :, 0:nt * 128]),
                  reads=['qr'], writes=['KTd'], dma='st_kt')
            def v_evac(t, bk, bn):
                P.add(ACT, R.copy(out=Vst[:, t, :, 0:128], in_=bk[:, :].rearrange("p (h d) -> p h d", h=4)), reads=[bn], writes=['Vst'])
            proj_tm(nt, 6, v_evac, ws)
            for h in range(4):
                P.add(SP, R.dma_start(out=Vd[h][:, kt0 * 129:(kt0 + nt) * 129].rearrange("p (t e) -> p t e", e=129), in_=Vst[:, 0:nt, h, :]),
                      reads=['Vst'], writes=['Vd'], dma='st_v')
            for t in reversed(range(nt)):
                if not is_ctx:
                    gt = tile0 + t
                    s4 = cnt['st'] % 4
                    cnt['st'] += 1
                    P.add(ACT, R.copy(out=Sbb[:, s4, :], in_=Sbs[:]), reads=['Sbs'], writes=['Sbb%d' % s4])
                    P.add(SP, R.dma_start(out=Sbd[gt], in_=Sbb[:, s4, :]), reads=['Sbb%d' % s4], writes=['Sbd%d' % gt], dma='st_sb%d' % s4)
                state_update(kr, rv, t, Sbs, 28, 'Sbs', ['kr', 'rv'])
            if is_ctx:
                for t in range(nt):
                    state_update(kr, vpf, t, Sf, 24, 'Sf', ['kr', 'vpf'])

        p1 = [(ctx_d, 2, True, 0, 0)] + [(x_d[n * 512:(n + 1) * 512, :], 4, False, n * 4, 2 + n * 4) for n in reversed(range(NB))]
        load_x(0, p1[0][0], p1[0][1])
        load_x(1, p1[1][0], p1[1][1])
        norm_stats(0, p1[0][1])
        for i, (src_, nt_, isc_, tile0_, kt0_) in enumerate(p1):
            def nxt(i=i):
                if i + 1 < len(p1):
                    norm_stats((i + 1) % 2, p1[i + 1][1])
                if i + 2 < len(p1):
                    load_x(i % 2, p1[i + 2][0], p1[i + 2][1])
            mod_part2_step(i)
            pass1_block(src_, nt_, isc_, tile0_, kt0_, i % 2, nxt, p1[i + 1][3] if i + 1 < len(p1) else None)
            ckpt('p1ctx' if i == 0 else 'p1b%d' % (NB - i))
        P.add(ACT, R.copy(out=Sfb[:, 0, :], in_=Sf[:]), reads=['Sf'], writes=['Sfb0'])

        def fm_proj_residual(slot, wsrc_fn, nk, rhs_fn, rhs_names, gcol, wview, ws_):
            xn = 'xblk%d' % slot
            for dt in range(8):
                wv, wn, col0 = wsrc_fn(dt)
                bi = dt % 2
                bk, bn = banks[bi], 'B%d' % bi
                for k in range(nk):
                    P.add(PE, R.matmul(out=bk[:, :], lhsT=wv[:, k, col0:col0 + 128], rhs=rhs_fn(k),
                                                                                start=(k == 0), stop=(k == nk - 1)), reads=[wn] + rhs_names, writes=[bn])
                ti = dt % 2
                P.add(ACT, R.activation(out=tmp[:, ti, :], in_=bk[:, :], func=AF.Identity, scale=gcol[:, dt:dt + 1]),
                      reads=[bn, 'cols', 'cols2'], writes=['tmp%d' % ti])
                b2, b2n = banks[2 + ti], 'B%d' % (2 + ti)
                for t in range(4):
                    P.add(PE, R.transpose(out=b2[:, t * 128:(t + 1) * 128], in_=tmp[:, ti, t * 128:(t + 1) * 128], identity=identf[:]),
                          reads=['tmp%d' % ti, 'identf'], writes=[b2n])
                P.add(DVE, R.tensor_tensor(out=xblk[:, slot, :, dt * 128:(dt + 1) * 128], in0=xblk[:, slot, :, dt * 128:(dt + 1) * 128],
                                                                   in1=b2[:, :].rearrange("p (t d) -> p t d", t=4), op=ALU.add), reads=[b2n, xn], writes=[xn])

        def ffn(n, ws_, hook=None):
            slot = n % 2
            hs = n % 2
            h2 = lambda k: h2T[:, hs, k, :]
            wupv = wupb_d.rearrange("(k p) c -> p k c", p=128)
            for fp in range(NF):
                if hook is not None:
                    hook(fp)
                par = fp % 2
                T = [tmp[:, 3 * par + i_, :] for i_ in range(3)]
                Tn = ['tmp%d' % (3 * par + i_) for i_ in range(3)]
                key = (n, fp // 2)
                if key not in ffn_w:
                    c0 = (fp // 2) * 256
                    ffn_w[key] = ws_.get(P, [(wupv[:, :, c0:c0 + 256], lambda v: v[:, :, 0:256]),
                                             (wupv[:, :, DFF + c0:DFF + c0 + 256], lambda v: v[:, :, 256:512])], _view_k8, res='wup_b')
                wv, wn = ffn_w[key]
                hb, hbn = banks[4 + par], 'B%d' % (4 + par)
                for ab in range(2):
                    ci = ab * 2 + (fp % 2)
                    bi = 2 * par + ab
                    bk, bn = banks[bi], 'B%d' % bi
                    hc = ab * 2
                    for k in range(8):
                        P.add(PE, R.matmul(out=bk[:, :], lhsT=wv[:, k, ci * 128:(ci + 1) * 128], rhs=h2(k)[:, 1:513],
                                           start=(k == 0), stop=(k == 7)), reads=[wn, 'h2T%d' % hs], writes=[bn])
                    for k in range(8):
                        P.add(PE, R.matmul(out=hb[:, hc:hc + 2], lhsT=wv[:, k, ci * 128:(ci + 1) * 128],
                                           rhs=h2(k)[:, 0:514:513], start=(k == 0), stop=(k == 7)),
                              reads=[wn, 'h2T%d' % hs], writes=[hbn])
                for ab in range(2):
                    f = fp + ab * NF
                    bi = 2 * par + ab
                    bk, bn = banks[bi], 'B%d' % bi
                    P.add(ACT, R.activation(out=T[ab], in_=bk[:, :], func=AF.Identity, bias=convb[:, f:f + 1], scale=convw[:, 1, f:f + 1]),
                          reads=[bn, 'convw', 'convb'], writes=[Tn[ab]])
                for ab in range(2):
                    f = fp + ab * NF
                    bi = 2 * par + ab
                    bk, bn = banks[bi], 'B%d' % bi
                    hc = ab * 2
                    tt, tn = T[ab], Tn[ab]
                    w0, w2 = convw[:, 0, f:f + 1], convw[:, 2, f:f + 1]
                    P.add(DVE, R.scalar_tensor_tensor(out=tt[:, 1:512], in0=bk[:, 0:511], scalar=w0, in1=tt[:, 1:512],
                                                      op0=ALU.mult, op1=ALU.add), reads=[bn, tn, 'convw'], writes=[tn])
                    P.add(DVE, R.scalar_tensor_tensor(out=tt[:, 0:511], in0=bk[:, 1:512], scalar=w2, in1=tt[:, 0:511],
                                                      op0=ALU.mult, op1=ALU.add), reads=[bn, tn, 'convw'], writes=[tn])
                    P.add(DVE, R.scalar_tensor_tensor(out=tt[:, 0:1], in0=hb[:, hc:hc + 1], scalar=w0, in1=tt[:, 0:1],
                                                      op0=ALU.mult, op1=ALU.add), reads=[hbn, tn, 'convw'], writes=[tn])
                    P.add(DVE, R.scalar_tensor_tensor(out=tt[:, 511:512], in0=hb[:, hc + 1:hc + 2], scalar=w2, in1=tt[:, 511:512],
                                                      op0=ALU.mult, op1=ALU.add), reads=[hbn, tn, 'convw'], writes=[tn])
                P.add(ACT, R.activation(out=T[2], in_=T[0], func=AF.Tanh, scale=0.5), reads=[Tn[0]], writes=[Tn[2]])
                P.add(POOL, R.tensor_tensor(out=T[2], in0=T[2], in1=T[0], op=ALU.mult), reads=[Tn[2], Tn[0]], writes=[Tn[2]])
                P.add(POOL, R.tensor_tensor(out=T[2], in0=T[2], in1=T[0], op=ALU.add), reads=[Tn[2], Tn[0]], writes=[Tn[2]])
                P.add(POOL, R.tensor_tensor(out=actT[:, fp, :], in0=T[2], in1=T[1], op=ALU.mult),
                      reads=[Tn[2], Tn[1]], writes=['actT'])
            wdv = wdnb_d.rearrange("(k p) c -> p k c", p=128)

            def wsrc(dt):
                wv, wn = ws_.get(P, wdv[:, :, dt * 128:(dt + 1) * 128], _view_dn, res='wdn_b')
                return wv, wn, 0
            fm_proj_residual(slot, wsrc, NF, lambda k: actT[:, k, :], ['actT'], cols[:, 7, :], _view_dn, ws_)
            xn = 'xblk%d' % slot
            for t in range(4):
                P.add(ACT, R.activation(out=tmp[:, 4, 0:512], in_=xblk[:, slot, t, 0:512], func=AF.Square,
                                                       accum_out=stat[:, 16 + 2 * t:17 + 2 * t]), reads=[xn], writes=['tmp4', 'stat'])
                P.add(ACT, R.activation(out=tmp[:, 4, 0:512], in_=xblk[:, slot, t, 512:1024], func=AF.Square,
                                                       accum_out=stat[:, 17 + 2 * t:18 + 2 * t]), reads=[xn], writes=['tmp4', 'stat'])
            P.add(DVE, R.tensor_reduce(out=stat[:, 24:28], in_=stat[:, 16:24].rearrange("p (t a) -> p t a", a=2), axis=AX.X, op=ALU.add),
                  reads=['stat'], writes=['stat'])
            rsqrt_col(stat[:, 24:28], stat[:, 24:28], 1.0 / D, EPS, 4)
            for t in range(4):
                P.add(DVE, R.scalar_tensor_tensor(out=xblk[:, slot, t, :], in0=xblk[:, slot, t, :], scalar=stat[:, 24 + t:25 + t],
                                                                                      in1=finT[:], op0=ALU.mult, op1=ALU.mult), reads=[xn, 'stat', 'finT'], writes=[xn])
            P.add(SP, R.dma_start(out=out_d[n * 512:(n + 1) * 512, :].rearrange("(t p) d -> p t d", p=128), in_=xblk[:, slot, :, :]),
                  reads=[xn], writes=['out'], dma='st_out%d' % slot)

        ffn_w = {}

        def load_kv(n, h):
            s = (n * 4 + h) % 2
            P.add(SP, R.dma_start(out=kvK(s), in_=KTd[h]), reads=['KTd'], writes=['kv%d' % s], dma='kvk%d' % s)
            P.add(SP, R.dma_start(out=kvV(s), in_=Vd[h].rearrange("p (k e) -> p k e", e=129)), reads=['Vd'], writes=['kv%d' % s], dma='kvv%d' % s)

        def prefetch(n):
            t0 = n * 512
            P.add(SP, R.dma_start(out=ropeR[:], in_=ropeR_d[t0:t0 + 512, :].rearrange("(t p) d -> p t d", p=128)), writes=['ropeR'], dma='ropeR')
            P.add(SP, R.dma_start(out=ropeD[:], in_=ropeD_d[t0:t0 + 512, :].rearrange("(t p) d -> p t d", p=128)), writes=['ropeD'], dma='ropeD')
            P.add(SP, R.dma_start(out=sbT[:], in_=Sbd[n * 4:(n + 1) * 4].rearrange("t p c -> p t c")), reads=['Sbd%d' % (n * 4 + i_) for i_ in range(4)], writes=['sbT'], dma='sbT')
            P.add(SP, R.dma_start(out=kr[:].rearrange("p t c -> p (t c)"), in_=KRd[n]), reads=['KRd%d' % n], writes=['kr'], dma='ld_kr')
            P.add(SP, R.dma_start(out=rv[:].rearrange("p t c -> p (t c)"), in_=RVd[n]), reads=['RVd%d' % n], writes=['rv'], dma='ld_rv')
            P.add(SP, R.dma_start(out=hT[:].rearrange("p k t -> p (k t)"), in_=HTd[n]), reads=['HTd%d' % n], writes=['hT'], dma='ld_ht')

        def attn_block(n):
            slot = n % 2
            xn = 'xblk%d' % slot
            t0 = n * 512
            load_x(slot, x_d[t0:t0 + 512, :], 4)
            ckpt('a%dn' % n)
            proj_tm(4, 0, rope_evac(qr, ropeR, 4, 64, 'qr', 'ropeR'), ws)
            ckpt('a%dp0' % n)
            ckpt('a%dp1' % n)
            for t_ in range(4):
                P.add(DVE, R.tensor_tensor(out=vpf[:, t_, :].rearrange("p (h d) -> p h d", h=4), in0=rv[:, t_, :].rearrange("p (h d) -> p h d", h=4),
                                           in1=bc(dect[:, 16:20].unsqueeze(2), [128, 4, 128]), op=ALU.mult), reads=['rv', 'dect'], writes=['vpf'])
            ckpt('a%dp2' % n)

            def gate_evac(t, bk, bn):
                P.add(ACT, R.activation(out=tmp[:, 4, :], in_=bk[:, :], func=AF.Tanh, scale=0.5), reads=[bn], writes=['tmp4'])
                P.add(DVE, R.scalar_tensor_tensor(out=sg[:, t, :], in0=tmp[:, 4, :], scalar=1.0, in1=bk[:, :], op0=ALU.add, op1=ALU.mult),
                      reads=[bn, 'tmp4'], writes=['sg'])
            proj_tm(4, 3, gate_evac, ws)
            dqb = Vst[:].rearrange("p a b c -> p (a b c)")[:, 0:2048].rearrange("p (t c) -> p t c", t=4)
            proj_tm(4, 4, rope_evac(dqb, ropeD, 8, 32, 'Vst', 'ropeD'), ws)
            ckpt('a%dp' % n)
            load_kv(n, 0)
            def ret_A(t):
                par = t % 2
                for src, dst, nm in ((qr, qT, 'qT%d' % par), (kr, kTt, 'kT%d' % par)):
                    for h in range(4):
                        P.add(PE, R.transpose(out=BT[:, h * 128:(h + 1) * 128], in_=src[:, t, h * 128:(h + 1) * 128], identity=identb[:]),
                              reads=['qr' if src is qr else 'kr', 'identb'], writes=['BT'])
                    if src is qr:
                        P.add(ACT, R.copy(out=dst[:, par, :], in_=BT[:, 0:512]), reads=['BT'], writes=[nm])
                    else:
                        P.add(DVE, R.tensor_copy(out=dst[:, par, :], in_=BT[:, 0:512]), reads=['BT'], writes=[nm])
                for h in range(4):
                    P.add(PE, R.matmul(out=banks[4][:, h * 128:(h + 1) * 128], lhsT=kTt[:, par, h * 128:(h + 1) * 128],
                                       rhs=qT[:, par, h * 128:(h + 1) * 128], start=True, stop=True),
                          reads=['qT%d' % par, 'kT%d' % par], writes=['B4'])
                P.add(DVE, R.tensor_tensor(out=AfT[:, par, :], in0=banks[4][:, :], in1=maskF[:].rearrange("p h i -> p (h i)"), op=ALU.mult),
                      reads=['B4', 'maskF'], writes=['AfT%d' % par])
                P.add(DVE, R.tensor_tensor(out=AbT[:, par, :], in0=banks[4][:, :], in1=maskB[:].rearrange("p h i -> p (h i)"), op=ALU.mult),
                      reads=['B4', 'maskB'], writes=['AbT%d' % par])
                fs = (n * 4 + t) % 2
                for (A, an, bki, Ssrc, sn) in ((AfT, 'AfT%d' % par, 2 * par, Sfb[:, fs, :], 'Sfb%d' % fs), (AbT, 'AbT%d' % par, 2 * par + 1, sbT[:, t, :], 'sbT')):
                    for h in range(4):
                        P.add(PE, R.matmul(out=banks[bki][:, h * 128:(h + 1) * 128], lhsT=A[:, par, h * 128:(h + 1) * 128],
                                           rhs=rv[:, t, h * 128:(h + 1) * 128], start=True, stop=False),
                              reads=[an, 'rv'], writes=['B%d' % bki])
                        P.add(PE, R.matmul(out=banks[bki][:, h * 128:(h + 1) * 128], lhsT=qT[:, par, h * 128:(h + 1) * 128],
                                           rhs=Ssrc[:, h * 128:(h + 1) * 128], start=False, stop=True),
                              reads=['qT%d' % par, sn], writes=['B%d' % bki])
                state_update(kr, vpf, t, Sf, 24, 'Sf', ['kr', 'vpf'])
                ns = (n * 4 + t + 1) % 2
                P.add(ACT, R.copy(out=Sfb[:, ns, :], in_=Sf[:]), reads=['Sf'], writes=['Sfb%d' % ns])

            def ret_B(t):
                par = t % 2
                O = tmp[:, 2 * par:2 * par + 2, :].rearrange("p a b -> p (a b)")
                On = ['tmp%d' % (2 * par), 'tmp%d' % (2 * par + 1)]
                O4 = O.rearrange("p (g e) -> p g e", e=128)
                SQ = tmp[:, 4:6, :].rearrange("p a b -> p (a b)")
                SQ4 = SQ.rearrange("p (g e) -> p g e", e=128)
                st = stat[:, 32 + 24 * par:56 + 24 * par]
                sn = 'statR%d' % par
                P.add(DVE, R.tensor_tensor(out=O4, in0=SC[:, par * 1024:(par + 1) * 1024].rearrange("p (g e) -> p g e", e=128),
                                           in1=bc(dect[:, 8:16].unsqueeze(2), [128, 8, 128]), op=ALU.mult),
                      reads=['B%d' % (2 * par), 'B%d' % (2 * par + 1), 'dect'], writes=On)
                P.add(DVE, R.tensor_reduce(out=st[:, 0:8], in_=O4, axis=AX.X, op=ALU.add), reads=On, writes=[sn])
                P.add(POOL, R.tensor_tensor(out=SQ, in0=O, in1=O, op=ALU.mult), reads=On, writes=['tmp4', 'tmp5'])
                P.add(DVE, R.tensor_reduce(out=st[:, 8:16], in_=SQ4, axis=AX.X, op=ALU.add), reads=['tmp4', 'tmp5'], writes=[sn, sn + 'q'])
                P.add(POOL, R.tensor_scalar(out=st[:, 0:8], in0=st[:, 0:8], scalar1=1.0 / 128, scalar2=None, op0=ALU.mult), reads=[sn], writes=[sn])
                P.add(POOL, R.tensor_tensor(out=st[:, 16:24], in0=st[:, 0:8], in1=st[:, 0:8], op=ALU.mult), reads=[sn], writes=[sn])
                P.add(DVE, R.scalar_tensor_tensor(out=st[:, 8:16], in0=st[:, 8:16], scalar=1.0 / 128, in1=st[:, 16:24], op0=ALU.mult, op1=ALU.subtract),
                      reads=[sn, sn + 'q'], writes=[sn, sn + 'q'])
                rsqrt_col(st[:, 8:16], st[:, 8:16], 1.0, GN_EPS, 8, sn)
                P.add(DVE, R.scalar_tensor_tensor(out=st[:, 16:24], in0=st[:, 0:8], scalar=-1.0, in1=st[:, 8:16], op0=ALU.mult, op1=ALU.mult),
                      reads=[sn], writes=[sn])
                for d_, eng_ in ((0, DVE), (1, POOL)):
                    Oh = O4[:, 4 * d_:4 * d_ + 4, :]
                    P.add(eng_, R.tensor_tensor(out=Oh, in0=Oh, in1=bc(st[:, 8 + 4 * d_:12 + 4 * d_].unsqueeze(2), [128, 4, 128]), op=ALU.mult),
                          reads=[On[d_], sn], writes=[On[d_]])
                    P.add(eng_, R.tensor_tensor(out=Oh, in0=Oh, in1=bc(st[:, 16 + 4 * d_:20 + 4 * d_].unsqueeze(2), [128, 4, 128]), op=ALU.add),
                          reads=[On[d_], sn], writes=[On[d_]])
                    P.add(eng_, R.tensor_tensor(out=O[:, 512 * d_:512 * d_ + 512], in0=O[:, 512 * d_:512 * d_ + 512], in1=gnT[:, 512 * d_:512 * d_ + 512], op=ALU.mult),
                          reads=[On[d_], 'gnT'], writes=[On[d_]])
                P.add(DVE, R.tensor_tensor(out=O[:, 0:512], in0=O[:, 0:512], in1=O[:, 512:1024], op=ALU.add), reads=On, writes=[On[0]])
                P.add(DVE, R.tensor_tensor(out=ybf[:, t, 0:512], in0=O[:, 0:512], in1=sg[:, t, :], op=ALU.mult), reads=[On[0], 'sg'], writes=['ybf'])

            def dq_transposes():
                for h in range(4):
                    for t in range(4):
                        P.add(PE, R.transpose(out=BT[:, t * 128:(t + 1) * 128], in_=dqb[:, t, h * 128:(h + 1) * 128], identity=identb[:]),
                              reads=['Vst', 'identb'], writes=['BT'])
                    P.add(ACT, R.copy(out=kr[:, h, :], in_=BT[:, 0:512]), reads=['BT'], writes=['kr'])
            ret_A(0)
            for t in range(4):
                if t + 1 < 4:
                    ret_A(t + 1)
                if t == 2:
                    dq_transposes()
                ret_B(t)
            ckpt('a%dr' % n)
            ckpt('a%dd' % n)
            def emit_qk(h, kt):
                s_ = (n * 4 + h) % 2
                Kh_ = kvK(s_)
                pb = (kt % 2) * 2
                for m in range(2):
                    P.add(PE, R.matmul(out=banks[pb + m][:, :], lhsT=Kh_[m * 64:(m + 1) * 64, kt * 128:(kt + 1) * 128],
                                       rhs=kr[m * 64:(m + 1) * 64, h, :], start=True, stop=True),
                          reads=['kv%d' % s_, 'kr'], writes=['B%d' % (pb + m)])

            accs = tmp[:, 0:3, :].rearrange("p a b -> p (a b)")[:, 0:8 * 129].rearrange("p (a e) -> p a e", e=129)
            accn = ['tmp0', 'tmp1', 'tmp2']
            emit_qk(0, 0)
            for h in range(4):
                s = (n * 4 + h) % 2
                if h < 3:
                    load_kv(n, h + 1)
                Vh = kvV(s)
                kvn = 'kv%d' % s
                for kt in range(KT_ALL):
                    pb = (kt % 2) * 2
                    P.add(ACT, R.activation(out=PT[:, pb:pb + 2, :].rearrange("p a b -> p (a b)"), in_=SC[:, pb * 512:(pb + 2) * 512],
                                            func=AF.Exp, scale=0.125),
                          reads=['B%d' % pb, 'B%d' % (pb + 1)], writes=['PT%d' % pb, 'PT%d' % (pb + 1)])
                    if kt + 1 < KT_ALL:
                        emit_qk(h, kt + 1)
                    elif h < 3:
                        emit_qk(h + 1, 0)
                    for qt in range(4):
                        for m in range(2):
                            a = qt * 2 + m
                            bki, co = 4 + a // 3, (a % 3) * 129
                            P.add(PE, R.matmul(
                                out=banks[bki][:, co:co + 129], lhsT=PT[:, pb + m, qt * 128:(qt + 1) * 128], rhs=Vh[:, kt, :],
                                start=(kt == 0 and a % 3 == 0), stop=(kt == KT_ALL - 1), skip_group_check=True),
                                reads=['PT%d' % (pb + m), kvn], writes=['B%d' % bki])
                for b_ in range(3):
                    na = 3 if b_ < 2 else 2
                    P.add(DVE, R.tensor_copy(out=accs[:, b_ * 3:b_ * 3 + na, :], in_=banks[4 + b_][:, 0:na * 129].rearrange("p (a e) -> p a e", e=129)),
                          reads=['B%d' % (4 + b_)], writes=accn)
                st = stat[:, 80:96]
                o3 = tmp[:, 3, :].rearrange("p (q e) -> p q e", q=4)
                sq3 = tmp[:, 4, :].rearrange("p (q e) -> p q e", q=4)
                P.add(DVE, R.reciprocal(out=st[:, 0:8], in_=accs[:, :, 128]), reads=accn, writes=['statD'])
                P.add(DVE, R.tensor_scalar(out=st[:, 1:8:2], in0=st[:, 1:8:2], scalar1=neglam, scalar2=None, op0=ALU.mult), reads=['statD', 'lams'], writes=['statD'])
                P.add(DVE, R.tensor_tensor(out=accs[:, :, 0:128], in0=accs[:, :, 0:128], in1=bc(st[:, 0:8].unsqueeze(2), [128, 8, 128]), op=ALU.mult),
                      reads=accn + ['statD'], writes=accn)
                P.add(POOL, R.tensor_tensor(out=o3, in0=accs[:, 0:8:2, 0:128], in1=accs[:, 1:8:2, 0:128], op=ALU.add), reads=accn, writes=['tmp3'])
                P.add(POOL, R.tensor_tensor(out=sq3, in0=o3, in1=o3, op=ALU.mult), reads=['tmp3'], writes=['tmp4'])
                P.add(DVE, R.tensor_reduce(out=st[:, 8:12], in_=sq3, axis=AX.X, op=ALU.add), reads=['tmp4'], writes=['statD'])
                rsqrt_col(st[:, 8:12], st[:, 8:12], 1.0 / 128, EPS, 4, 'statD')
                P.add(DVE, R.tensor_tensor(out=o3, in0=o3, in1=bc(st[:, 8:12].unsqueeze(2), [128, 4, 128]), op=ALU.mult), reads=['tmp3', 'statD'], writes=['tmp3'])
                P.add(POOL, R.tensor_tensor(out=ybf[:, :, 512 + h * 128:640 + h * 128], in0=o3,
                                            in1=bc(subT[:, h * 128:(h + 1) * 128].unsqueeze(1), [128, 4, 128]), op=ALU.mult),
                      reads=['tmp3', 'subT'], writes=['ybf'])
            ckpt('a%dx' % n)
            for c in range(8):
                for t in range(4):
                    P.add(PE, R.transpose(out=BT[:, t * 128:(t + 1) * 128], in_=ybf[:, t, c * 128:(c + 1) * 128], identity=identb[:]),
                          reads=['ybf', 'identb'], writes=['BT'])
                P.add(ACT, R.copy(out=hT[:, c, :], in_=BT[:, 0:512]), reads=['BT'], writes=['hT'])
            wov = woutb_d.rearrange("(k p) c -> p k c", p=128)
            wo = {}

            def wsrc(dt):
                g = dt // 4
                if g not in wo:
                    wo[g] = ws.get(P, wov[:, :, g * 512:(g + 1) * 512], _view_k8, res='wout_b')
                return wo[g][0], wo[g][1], (dt % 4) * 128
            fm_proj_residual(slot, wsrc, 8, lambda k: hT[:, k, :], ['hT'], cols[:, 6, :], _view_k8, ws)
            if n + 1 < NB:
                prefetch(n + 1)
            ckpt('a%do' % n)
            hs = n % 2
            norm_to_T(slot, 4, lambda k: h2T[:, hs, k, :], 1, 'h2T%d' % hs, cols[:, 4, :], cols[:, 5, :])
            if n > 0:
                P.add(POOL, R.tensor_copy(out=h2T[:, 1 - hs, :, 513:514], in_=h2T[:, hs, :, 1:2]), reads=['h2T%d' % hs], writes=['h2T%d' % (1 - hs)])
                P.add(POOL, R.tensor_copy(out=h2T[:, hs, :, 0:1], in_=h2T[:, 1 - hs, :, 512:513]), reads=['h2T%d' % (1 - hs)], writes=['h2T%d' % hs])
            else:
                P.add(POOL, R.memset(h2T[:, hs, :, 0:1], 0.0), writes=['h2T%d' % hs])
            if n == NB - 1:
                P.add(POOL, R.memset(h2T[:, hs, :, 513:514], 0.0), writes=['h2T%d' % hs])

        mod_finish()
        ckpt('p1')
        prefetch(0)
        for n in range(NB):
            attn_block(n)
            ckpt('a%d' % n)
            hook = None
            if n > 0:
                ffn(n - 1, ws, hook)
                ckpt('f%d' % (n - 1))
        ffn(NB - 1, ws)

    Pd = Prog(nc)
    wsd = WStream(ring, 2)
    try:
        gen(Pd, wsd)
    except _Stop:
        pass
    P = Prog(nc)
    ws = WStream(ring, 2, plan=wsd.rec)
    try:
        gen(P, ws)
    except _Stop:
        pass
    P.emit()
    return nc, P


_CACHE = {}


def _consts():
    f32 = np.float32
    ident = np.eye(128, dtype=f32)
    j = np.arange(128)[:, None]
    i = np.arange(128)[None, :]
    cmask = np.concatenate([(j <= i).astype(f32), (j >= i).astype(f32)], axis=1)
    p = np.arange(128, dtype=f32)[:, None]
    one4 = np.ones((1, 4), f32)
    ksc = np.concatenate([(p + 1) * one4, (128 - p) * one4], axis=1)
    kdec = np.concatenate([-(127 - p) * one4, -p * one4], axis=1)
    cdec = np.full((128, 8), -128.0, f32)
    decexp = np.concatenate([ksc, -ksc, kdec, cdec], axis=1).astype(f32)

    def rope(hd):
        nf = hd // 4
        inv = (10000.0 ** (-(np.arange(nf, dtype=f32) / f32(nf)))).astype(f32)
        pos = np.arange(L)
        row = (pos // 64).astype(f32)
        col = (pos % 64).astype(f32)
        ang = np.concatenate([row[:, None] * inv, col[:, None] * inv], axis=-1).astype(f32)
        return np.concatenate([np.cos(ang), np.sin(ang)], axis=-1).astype(f32)
    return dict(ident=ident, cmask=cmask, decexp=decexp, rope_r=rope(128), rope_d=rope(64))


def _colmajor(v, k):
    return np.ascontiguousarray(np.asarray(v, np.float32).reshape(k, 128).T)


def kernel(x, c, ctx, c_ctx, w_mod, b_mod, norm1_g, w_in, ret_decay_logit, ret_gn_g,
           diff_lambda, diff_subln_g, w_out, norm2_g, w_up, conv_w, conv_b, w_down, final_g):
    if 'nc' not in _CACHE:
        _CACHE['nc'] = build_program()
    nc, _ = _CACHE['nc']
    in_maps = _prep(x, c, ctx, c_ctx, w_mod, b_mod, norm1_g, w_in, ret_decay_logit, ret_gn_g,
                    diff_lambda, diff_subln_g, w_out, norm2_g, w_up, conv_w, conv_b, w_down, final_g)
    res = run_bass_kernel_spmd(nc, in_maps, core_ids=list(range(8)))
    return np.stack([np.asarray(r['out'], np.float32) for r in res.results], axis=0)


def _prep(x, c, ctx, c_ctx, w_mod, b_mod, norm1_g, w_in, ret_decay_logit, ret_gn_g,
          diff_lambda, diff_subln_g, w_out, norm2_g, w_up, conv_w, conv_b, w_down, final_g):
    cst = _consts()
    f32 = np.float32
    x = np.asarray(x, f32)
    ctx = np.asarray(ctx, f32)
    c = np.asarray(c, f32)
    shared = dict(
        w_mod=np.ascontiguousarray(np.asarray(w_mod, f32)[0]),
        bmodc=_colmajor(np.asarray(b_mod)[0], 48),
        n1g=_colmajor(np.asarray(norm1_g)[0], 8),
        n2g=_colmajor(np.asarray(norm2_g)[0], 8),
        w_in=np.ascontiguousarray(np.asarray(w_in, f32)[0]),
        dlog=np.ascontiguousarray(np.asarray(ret_decay_logit, f32)[0].reshape(1, 8)),
        gng=np.ascontiguousarray(np.asarray(ret_gn_g, f32)[0].reshape(1, 1024)),
        dlam=np.ascontiguousarray(np.asarray(diff_lambda, f32)[0].reshape(1, 256)),
        subg=np.ascontiguousarray(np.asarray(diff_subln_g, f32)[0].reshape(1, 512)),
        w_out=np.ascontiguousarray(np.asarray(w_out, f32)[0]),
        w_up=np.ascontiguousarray(np.asarray(w_up, f32)[0]),
        convwc=np.ascontiguousarray(np.asarray(conv_w, f32)[0].reshape(3, 44, 128).transpose(2, 0, 1).reshape(128, 132)),
        convbc=_colmajor(np.asarray(conv_b)[0], 44),
        w_down=np.ascontiguousarray(np.asarray(w_down, f32)[0]),
        fing=np.ascontiguousarray(np.asarray(final_g, f32).reshape(1, D)),
        **cst,
    )
    cc = _colmajor(np.asarray(c_ctx), 8)
    in_maps = []
    for b in range(8):
        m = dict(shared)
        m['x'] = np.ascontiguousarray(x[b])
        m['ctx'] = np.ascontiguousarray(ctx[b])
        m['ccol'] = np.ascontiguousarray(np.concatenate([_colmajor(c[b], 8), cc], axis=1))
        in_maps.append(m)
    return in_maps
```
